# Optimizing a Trainium2 kernel written in Bass

```python
import math
import jax, jax.numpy as jnp
from jax import lax
import numpy as np

D_MODEL = 1024
BATCH = 4
SEQ = 4096
DEPTH = 4
DEC_BATCH = 8
DEC_SEQ = 2048
PAST_LEN = 128

N_MIXERS = 2
N_FOURIER_LAYERS = (DEPTH + 1) // 2
N_GDN_LAYERS = DEPTH // 2
FNET_GROUP_DIM = 128
FNET_GROUPS = D_MODEL // FNET_GROUP_DIM
GDN_HEAD_DIM = 128
GDN_HEADS = D_MODEL // GDN_HEAD_DIM
GDN_QK_DIM = GDN_HEADS * GDN_HEAD_DIM
GDN_V_DIM = GDN_HEADS * GDN_HEAD_DIM
GDN_QKV_DIM = 2 * GDN_QK_DIM + GDN_V_DIM
GDN_PROJ_DIM = GDN_QKV_DIM + GDN_V_DIM + 4 * GDN_HEADS
CONV_K = 5
CHUNK = 64
FFN_DIM = ((8 * D_MODEL + 3 * 256 - 1) // (3 * 256)) * 256
EPS = 1e-6

kernel_name = 'hybrid_fnet_gdn_adaln_encoder'


def rmsnorm(x, g):
    xf = x.astype(jnp.float32)
    y = xf * lax.rsqrt(jnp.mean(xf * xf, axis=-1, keepdims=True) + EPS)
    return (y * g.astype(jnp.float32)).astype(x.dtype)


def l2norm(t):
    return t * lax.rsqrt(jnp.sum(t * t, axis=-1, keepdims=True) + EPS)


def fourier_mixer(h, w, b):
    bsz, s, d = h.shape
    hf = h.astype(jnp.float32).reshape(bsz, s, FNET_GROUPS, FNET_GROUP_DIM)
    mixed = jnp.fft.fft2(hf, axes=(1, 3), norm='ortho').real.reshape(bsz, s, d).astype(h.dtype)
    return mixed @ w + b


def centred_depthwise_conv(x, w):
    ch = x.shape[-1]
    return lax.conv_general_dilated(
        x, w[:, None, :].astype(x.dtype), window_strides=(1,),
        padding=[(CONV_K // 2, CONV_K // 2)],
        dimension_numbers=('NWC', 'WIO', 'NWC'), feature_group_count=ch)


def gated_delta_chunked(q, k, v, g, beta):
    bsz, s, nh, dk = q.shape
    dv = v.shape[-1]
    n = s // CHUNK

    def chunkify(t):
        return t.reshape(bsz, n, CHUNK, nh, -1).transpose(0, 3, 1, 2, 4)

    q, k, v = chunkify(q), chunkify(k), chunkify(v)
    g = g.reshape(bsz, n, CHUNK, nh).transpose(0, 3, 1, 2)
    beta = beta.reshape(bsz, n, CHUNK, nh).transpose(0, 3, 1, 2)
    gc = jnp.cumsum(g, axis=-1)
    tril = jnp.tril(jnp.ones((CHUNK, CHUNK), dtype=bool))
    strict = jnp.tril(jnp.ones((CHUNK, CHUNK), dtype=bool), -1)
    diff = gc[..., :, None] - gc[..., None, :]
    decay_mat = jnp.exp(jnp.where(tril, diff, -jnp.inf))
    kb = k * beta[..., None]
    a_mat = jnp.where(strict, jnp.einsum('bhnid,bhnjd->bhnij', kb, k) * decay_mat, 0.0)
    ia = a_mat + jnp.eye(CHUNK, dtype=jnp.float32)
    rhs = jnp.concatenate([v * beta[..., None], kb * jnp.exp(gc)[..., None]], axis=-1)
    sol = lax.linalg.triangular_solve(ia, rhs, left_side=True, lower=True, unit_diagonal=True)
    u, w = sol[..., :dv], sol[..., dv:]
    qk = jnp.einsum('bhnid,bhnjd->bhnij', q, k) * decay_mat
    q_dec = q * jnp.exp(gc)[..., None]
    k_dec = k * jnp.exp(gc[..., -1:] - gc)[..., None]
    g_last = jnp.exp(gc[..., -1])

    def step(state, xs):
        qk_i, qd_i, kd_i, u_i, w_i, gl_i = xs
        delta = u_i - jnp.einsum('bhcd,bhdv->bhcv', w_i, state)
        o = jnp.einsum('bhcd,bhdv->bhcv', qd_i, state) + jnp.einsum('bhij,bhjv->bhiv', qk_i, delta)
        state = state * gl_i[..., None, None] + jnp.einsum('bhcd,bhcv->bhdv', kd_i, delta)
        return state, o

    xs = tuple(jnp.moveaxis(t, 2, 0) for t in (qk, q_dec, k_dec, u, w, g_last))
    s0 = jnp.zeros((bsz, nh, dk, dv), jnp.float32)
    _, o = lax.scan(step, s0, xs)
    return o.transpose(1, 0, 3, 2, 4).reshape(bsz, s, nh, dv)


def gdn_mixer(h, w_in, conv_w, a_log, dt_bias, norm_g, w_out):
    bsz, s, _ = h.shape
    proj = h @ w_in
    qkv = jax.nn.silu(centred_depthwise_conv(proj[..., :GDN_QKV_DIM], conv_w))
    z = proj[..., GDN_QKV_DIM:GDN_QKV_DIM + GDN_V_DIM].astype(jnp.float32).reshape(bsz, s, GDN_HEADS, GDN_HEAD_DIM)
    ab = proj[..., GDN_QKV_DIM + GDN_V_DIM:].astype(jnp.float32).reshape(bsz, s, 4, GDN_HEADS)
    qkv = qkv.astype(jnp.float32)
    q = qkv[..., :GDN_QK_DIM].reshape(bsz, s, GDN_HEADS, GDN_HEAD_DIM)
    k = qkv[..., GDN_QK_DIM:2 * GDN_QK_DIM].reshape(bsz, s, GDN_HEADS, GDN_HEAD_DIM)
    v = qkv[..., 2 * GDN_QK_DIM:].reshape(bsz, s, GDN_HEADS, GDN_HEAD_DIM)
    q = l2norm(q) * (GDN_HEAD_DIM ** -0.5)
    k = l2norm(k)
    g = -jnp.exp(a_log.astype(jnp.float32)) * jax.nn.softplus(ab[:, :, 0:2] + dt_bias.astype(jnp.float32))
    beta = jax.nn.sigmoid(ab[:, :, 2:4])
    o_fwd = gated_delta_chunked(q, k, v, g[:, :, 0], beta[:, :, 0])
    o_bwd = gated_delta_chunked(q[:, ::-1], k[:, ::-1], v[:, ::-1], g[:, ::-1, 1], beta[:, ::-1, 1])[:, ::-1]
    o = o_fwd + o_bwd
    o = o * lax.rsqrt(jnp.mean(o * o, axis=-1, keepdims=True) + EPS) * norm_g.astype(jnp.float32) * jax.nn.silu(z)
    return o.reshape(bsz, s, GDN_V_DIM).astype(h.dtype) @ w_out


def swiglu(h, w_gu, w_down):
    gu = h @ w_gu
    return (jax.nn.silu(gu[..., :FFN_DIM]) * gu[..., FFN_DIM:]) @ w_down


def trunk(x, c, ada_w, ada_b, norm_mix_g, norm_ffn_g, fnet_w, fnet_b, gdn_w_in, gdn_conv_w,
          gdn_a_log, gdn_dt_bias, gdn_norm_g, gdn_w_out, ffn_w_gu, ffn_w_down,
          final_ada_w, final_ada_b, final_norm_g):
    cs = jax.nn.silu(c)
    for layer in range(DEPTH):
        mod = (cs @ ada_w[layer] + ada_b[layer])[:, None, :]
        sh1, sc1, g1, sh2, sc2, g2 = jnp.split(mod, 6, axis=-1)
        h = rmsnorm(x, norm_mix_g[layer]) * (1.0 + sc1) + sh1
        idx = layer // N_MIXERS
        if layer % N_MIXERS == 0:
            y = fourier_mixer(h, fnet_w[idx], fnet_b[idx])
        else:
            y = gdn_mixer(h, gdn_w_in[idx], gdn_conv_w[idx], gdn_a_log[idx], gdn_dt_bias[idx],
                          gdn_norm_g[idx], gdn_w_out[idx])
        x = x + g1 * y
        h = rmsnorm(x, norm_ffn_g[layer]) * (1.0 + sc2) + sh2
        x = x + g2 * swiglu(h, ffn_w_gu[layer], ffn_w_down[layer])
    fmod = (cs @ final_ada_w + final_ada_b)[:, None, :]
    sh, sc = jnp.split(fmod, 2, axis=-1)
    return rmsnorm(x, final_norm_g) * (1.0 + sc) + sh


def setup_inputs(seed: int = 0) -> dict:
    key = jax.random.key(seed)
    ks = jax.random.split(key, 24)
    D = D_MODEL

    def nrm(k, shape, scale):
        return jax.random.normal(k, shape, jnp.float32) * scale

    dt = jnp.exp(jax.random.uniform(ks[13], (N_GDN_LAYERS, 2, GDN_HEADS), jnp.float32,
                                    minval=math.log(1e-3), maxval=math.log(1e-1)))
    return {
        'x_prompt': nrm(ks[0], (BATCH, SEQ, D), 1.0),
        'x_sample': nrm(ks[1], (DEC_BATCH, DEC_SEQ, D), 1.0),
        'c_prompt': nrm(ks[2], (BATCH, D), 1.0),
        'c_sample': nrm(ks[3], (DEC_BATCH, D), 1.0),
        'ada_w': nrm(ks[4], (DEPTH, D, 6 * D), 0.5 * D ** -0.5),
        'ada_b': nrm(ks[5], (DEPTH, 6 * D), 0.02),
        'norm_mix_g': 1.0 + nrm(ks[6], (DEPTH, D), 0.02),
        'norm_ffn_g': 1.0 + nrm(ks[7], (DEPTH, D), 0.02),
        'fnet_w': nrm(ks[8], (N_FOURIER_LAYERS, D, D), D ** -0.5),
        'fnet_b': nrm(ks[9], (N_FOURIER_LAYERS, D), 0.02),
        'gdn_w_in': nrm(ks[10], (N_GDN_LAYERS, D, GDN_PROJ_DIM), D ** -0.5),
        'gdn_conv_w': nrm(ks[11], (N_GDN_LAYERS, CONV_K, GDN_QKV_DIM), CONV_K ** -0.5),
        'gdn_a_log': jnp.log(jax.random.uniform(ks[12], (N_GDN_LAYERS, 2, GDN_HEADS), jnp.float32,
                                                minval=1.0, maxval=16.0)),
        'gdn_dt_bias': dt + jnp.log(-jnp.expm1(-dt)),
        'gdn_norm_g': 1.0 + nrm(ks[14], (N_GDN_LAYERS, GDN_HEAD_DIM), 0.02),
        'gdn_w_out': nrm(ks[15], (N_GDN_LAYERS, GDN_V_DIM, D), GDN_V_DIM ** -0.5),
        'ffn_w_gu': nrm(ks[16], (DEPTH, D, 2 * FFN_DIM), D ** -0.5),
        'ffn_w_down': nrm(ks[17], (DEPTH, FFN_DIM, D), FFN_DIM ** -0.5),
        'final_ada_w': nrm(ks[18], (D, 2 * D), 0.5 * D ** -0.5),
        'final_ada_b': nrm(ks[19], (2 * D,), 0.02),
        'final_norm_g': 1.0 + nrm(ks[20], (D,), 0.02),
    }


def reference(x_prompt, x_sample, c_prompt, c_sample, ada_w, ada_b, norm_mix_g, norm_ffn_g,
              fnet_w, fnet_b, gdn_w_in, gdn_conv_w, gdn_a_log, gdn_dt_bias, gdn_norm_g, gdn_w_out,
              ffn_w_gu, ffn_w_down, final_ada_w, final_ada_b, final_norm_g):
    y_prompt = trunk(x_prompt, c_prompt, ada_w, ada_b, norm_mix_g, norm_ffn_g, fnet_w, fnet_b,
                     gdn_w_in, gdn_conv_w, gdn_a_log, gdn_dt_bias, gdn_norm_g, gdn_w_out,
                     ffn_w_gu, ffn_w_down, final_ada_w, final_ada_b, final_norm_g)
    y_sample = trunk(x_sample, c_sample, ada_w, ada_b, norm_mix_g, norm_ffn_g, fnet_w, fnet_b,
                     gdn_w_in, gdn_conv_w, gdn_a_log, gdn_dt_bias, gdn_norm_g, gdn_w_out,
                     ffn_w_gu, ffn_w_down, final_ada_w, final_ada_b, final_norm_g)
    return (y_prompt, y_sample)
```

```python
import contextlib
import os
import numpy as np
import ml_dtypes
import concourse.bass as bass
import concourse.mybir as mybir
from concourse.bass_utils import run_bass_kernel_spmd

F32, BF16 = mybir.dt.float32, mybir.dt.bfloat16
F32R = mybir.dt.float32r
AF = mybir.ActivationFunctionType
ALU = mybir.AluOpType
AX = mybir.AxisListType

D = 1024
NCH = 8
FF = 2816
NFC = 22
EPS = 1e-6
GP = 4128
CH = 64


class Eng:
    def __init__(self, name, h, sem):
        self.name, self.h, self.sem = name, h, sem
        self.count = 0
        self.seen = {}


class DSem:
    def __init__(self, sem):
        self.sem = sem
        self.count = 0
        self.last = None


class Buf:
    __slots__ = ("w", "r", "name", "excl")

    def __init__(self, name, init=None):
        self.name = name
        self.excl = False
        self.w = {}
        self.r = dict(init) if init else {}


class K:
    def __init__(self, nc, es, ndma=24):
        self.nc = nc
        self.es = es
        mk = lambda n: es.enter_context(nc.semaphore(n))
        self.eng = {
            "pe": Eng("pe", nc.tensor, mk("s_pe")),
            "act": Eng("act", nc.scalar, mk("s_act")),
            "dve": Eng("dve", nc.vector, mk("s_dve")),
            "pool": Eng("pool", nc.gpsimd, mk("s_pool")),
            "sp": Eng("sp", nc.sync, mk("s_sp")),
        }
        self.dsems = {q: [DSem(mk(f"d_{q}{i}")) for i in range(ndma if q != "act" else 8)] for q in ("sp", "pool", "act")}
        self.di = {"sp": 0, "pool": 0, "act": 0}
        self.all_bufs = []
        self.phase_tok = {}

    def buf(self, name):
        b = Buf(name, self.phase_tok)
        self.all_bufs.append(b)
        return b

    def end_phase(self, bufs):
        tok = dict(self.phase_tok)
        for b in bufs:
            for s, v in list(b.w.items()) + list(b.r.items()):
                tok[s] = max(tok.get(s, 0), v)
        self.phase_tok = tok

    def _wait(self, E, deps):
        for s, v in deps.items():
            if s is E and E.name == "pe":
                continue
            if E.seen.get(s, 0) >= v:
                continue
            E.h.wait_ge(s.sem, v)
            E.seen[s] = v

    def _deps(self, reads, writes, par=False):
        deps = {}

        def add(s, v):
            if deps.get(s, 0) < v:
                deps[s] = v

        for b in reads:
            for s, v in b.w.items():
                add(s, v)
        for b in writes:
            for s, v in b.w.items():
                if par and isinstance(s, DSem):
                    continue
                add(s, v)
            for s, v in b.r.items():
                add(s, v)
        return deps

    def _mark(self, tok, reads, writes, par=False):
        s, v = tok
        for b in reads:
            if b.r.get(s, 0) < v:
                b.r[s] = v
        for b in writes:
            if par:
                b.w[s] = v
            else:
                b.w = {s: v}
            b.r = {}

    def op(self, e, fn, reads=(), writes=()):
        E = self.eng[e]
        ex = [b for b in reads if b.excl]
        if ex:
            writes = list(writes) + [b for b in ex if b not in writes]
            reads = [b for b in reads if not b.excl]
        self._wait(E, self._deps(reads, writes))
        ins = fn(E.h)
        E.count += 1
        ins.then_inc(E.sem, 1)
        tok = (E, E.count)
        self._mark(tok, reads, writes)
        return tok

    def dma(self, q, out, in_, reads=(), writes=(), par=False):
        E = self.eng[q]
        ds = self.dsems[q][self.di[q] % len(self.dsems[q])]
        self.di[q] += 1
        deps = self._deps(reads, writes, par)
        if ds.count:
            deps[ds] = max(deps.get(ds, 0), ds.count)
        self._wait(E, deps)
        ins = E.h.dma_start(out=out, in_=in_)
        ds.count += 16
        ins.then_inc(ds.sem, 16)
        tok = (ds, ds.count)
        self._mark(tok, reads, writes, par)
        return tok

    def finish(self):
        E = self.eng["sp"]
        deps = {}
        for q in self.dsems:
            for ds in self.dsems[q]:
                if ds.count:
                    deps[ds] = ds.count
        for n, e in self.eng.items():
            if e.count and n != "sp":
                deps[e] = e.count
        self._wait(E, deps)


def build(NT, layer_types, n_final=True):
    nc = bass.Bass("TRN2", target_bir_lowering=False)
    L = len(layer_types)
    NTT = NT // 512
    NB = NT // 128
    SL = NT // 2
    nF = max(1, sum(1 for t in layer_types if t == "f"))
    nG = max(1, sum(1 for t in layer_types if t == "g"))

    def din(name, shape, dt=F32):
        return nc.dram_tensor(name, list(shape), dt, kind="ExternalInput").ap()

    def dscr(name, shape, dt=F32):
        return nc.dram_tensor(name, list(shape), dt, kind="Internal").ap()

    x_in = din("x_in", [NT, D])
    cT = din("cT", [128, NCH, 2])
    ada_w = din("ada_w", [L, D, 6 * D])
    ada_bT = din("ada_bT", [L, 128, 48])
    nmgT = din("nmgT", [L, 128, NCH])
    nfgT = din("nfgT", [L, 128, NCH])
    fada_w = din("fada_w", [D, 2 * D])
    fada_bT = din("fada_bT", [128, 16])
    fngT = din("fngT", [128, NCH])
    fnet_w = din("fnet_w", [nF, D, D])
    fnet_bT = din("fnet_bT", [nF, 128, NCH])
    gdn_w_in = din("gdn_w_in", [nG, D, GP])
    conv_wT = din("conv_wT", [nG, 128, 24, 5])
    alog_row = din("alog_row", [nG, 16])
    dtb_col = din("dtb_col", [nG, 32, 1])
    gng_col = din("gng_col", [nG, 128, 1])
    gdn_w_out = din("gdn_w_out", [nG, D, D])
    ffn_w_gu = din("ffn_w_gu", [L, D, 2 * FF])
    ffn_w_down = din("ffn_w_down", [L, FF, D])
    dftc = din("dftc", [NB, 128, NB, 128], BF16)
    dfts = din("dfts", [NB, 128, NB, 128], BF16)
    cdft = din("cdft", [128, 256], BF16)
    cf32 = din("cf32", [128, 8, 128])
    flag = din("flag", [128, 1])
    y_out = nc.dram_tensor("y_out", [NT, D], F32, kind="ExternalOutput").ap()

    xT = dscr("xT", [NCH, 128, NT])
    P_s = dscr("P_s", [24, 128, NT])
    qT_s = dscr("qT_s", [8, 128, NT], BF16)
    kT_s = dscr("kT_s", [8, 128, NT], BF16)
    zT_s = dscr("zT_s", [8, 128, NT], BF16)
    ktm_s = dscr("ktm_s", [NT, D], BF16)
    vtm_s = dscr("vtm_s", [NT, D], BF16)
    gb_s = dscr("gb_s", [NT, 32])
    o_s = dscr("o_s", [NT, D])

    es = contextlib.ExitStack()
    with es:
        k = K(nc, es)
        op, dma = k.op, k.dma

        uniq = [0]

        def sb(stack, name, shape, dt=F32):
            uniq[0] += 1
            return stack.enter_context(nc.sbuf_tensor(f"{name}_{uniq[0]}", list(shape), dt))

        cst = sb(es, "cst", [128, 8, 128]); b_cst = k.buf("cst")
        cstb = sb(es, "cstb", [128, 8, 128], BF16); b_cstb = k.buf("cstb")
        mods = sb(es, "mods", [128, L + 1, 72, 2]); b_mods = k.buf("mods")
        csT = sb(es, "csT", [128, NCH, 2], BF16); b_csT = k.buf("csT")
        flg = sb(es, "flg", [128, 1]); b_flg = k.buf("flg")
        epsc = sb(es, "epsc", [128, 1]); b_eps = k.buf("epsc")
        pbank = [es.enter_context(nc.psum_tensor(f"pb{i}", [128, 512], F32)) for i in range(8)]
        b_pb = [k.buf(f"pb{i}") for i in range(8)]
        for b_ in b_pb:
            b_.excl = True

        C_ID, C_ONES, C_MCF, C_MCB, C_M2F, C_M2B, C_NMF, C_NMB = range(8)
        dma("sp", cst[:], cf32, writes=[b_cst])
        dma("sp", flg[:], flag, writes=[b_flg])
        op("dve", lambda e: e.tensor_copy(out=cstb[:], in_=cst[:]), reads=[b_cst], writes=[b_cstb])
        op("pool", lambda e: e.memset(epsc[:], EPS), writes=[b_eps])
        ident_b = cstb[:, C_ID, :]
        ident_f = cst[:, C_ID, :]
        ones_b = cstb[:, C_ONES, :]
        ones_f = cst[:, C_ONES, :]

        b_Ps = [k.buf(f"Ps{t}") for t in range(NTT)]; b_zs = [k.buf(f"zs{t}") for t in range(NTT)]
        b_gbs = [k.buf(f"gbs{t}") for t in range(NTT)]; b_qs = [k.buf(f"qs{t}") for t in range(NTT)]
        b_ks = [k.buf(f"ks{t}") for t in range(NTT)]; b_ktm = [k.buf(f"ktm{t}") for t in range(NTT)]
        b_vtm = [k.buf(f"vtm{t}") for t in range(NTT)]; b_os = [k.buf(f"os{t}") for t in range(NTT)]
        b_xTt = [[k.buf(f"xT{t}_{c}") for c in range(NCH)] for t in range(NTT)]
        with contextlib.ExitStack() as ph:
            pbufs = []
            def pbuf(n):
                b = k.buf(n); pbufs.append(b); return b
            ct = sb(ph, "ct", [128, NCH, 2]); b_ct = pbuf("ct")
            dma("sp", ct[:], cT, writes=[b_ct])
            op("act", lambda e: e.activation(out=csT[:], in_=ct[:], func=AF.Silu), reads=[b_ct], writes=[b_csT])
            wb = [sb(ph, f"adaw{i}", [128, NCH, 768], BF16) for i in range(2)]
            b_wb = [pbuf(f"adaw{i}") for i in range(2)]
            abt = sb(ph, "abt", [128, L + 1, 48]); b_abt = pbuf("abt")
            gT = sb(ph, "gT", [128, L + 1, 2, NCH]); b_gT = pbuf("gT")
            fbT = sb(ph, "fbT", [128, nF, NCH]); b_fbT = pbuf("fbT")
            op("pool", lambda e: e.memset(abt[:], 0.0), writes=[b_abt])
            for l in range(L):
                dma("sp", abt[:, l, :], ada_bT[l], writes=[b_abt])
                dma("sp", gT[:, l, 0, :], nmgT[l], writes=[b_gT])
                dma("sp", gT[:, l, 1, :], nfgT[l], writes=[b_gT])
            dma("sp", abt[:, L, 0:16], fada_bT, writes=[b_abt])
            dma("sp", gT[:, L, 0, :], fngT, writes=[b_gT])
            dma("sp", gT[:, L, 1, :], fngT, writes=[b_gT])
            for i in range(nF):
                dma("sp", fbT[:, i, :], fnet_bT[i], writes=[b_fbT])
            wst = [sb(ph, f"adast{i}", [128, NCH, 768]) for i in range(2)]
            b_wst = [pbuf(f"adast{i}") for i in range(2)]
            xin = [sb(ph, f"xin{i}", [128, D]) for i in range(2)]
            b_xin = [pbuf(f"xin{i}") for i in range(2)]
            xst = [sb(ph, f"xst{i}", [128, NCH, 512]) for i in range(2)]
            b_xst = [pbuf(f"xst{i}") for i in range(2)]

            def emit_xblock(tb):
                xi = xin[tb % 2]; bxi = b_xin[tb % 2]
                dma("pool", xi[:], x_in[tb * 128:(tb + 1) * 128, :], writes=[bxi])
                st = xst[(tb // 4) % 2]; bst = b_xst[(tb // 4) % 2]
                for half in range(2):
                    pb_i = 2 + (tb * 2 + half) % 4
                    for c4 in range(4):
                        c = half * 4 + c4
                        op("pe", lambda e, xi=xi, c=c, c4=c4, pb_i=pb_i: e.transpose(
                            out=pbank[pb_i][:, c4 * 128:(c4 + 1) * 128], in_=xi[:, c * 128:(c + 1) * 128], identity=ident_f),
                           reads=[bxi, b_cst], writes=[b_pb[pb_i]])
                    if half == 0:
                        op("act", lambda e, st=st, half=half, pb_i=pb_i, tb=tb: e.copy(
                            out=st[:, half * 4:half * 4 + 4, (tb % 4) * 128:(tb % 4 + 1) * 128],
                            in_=pbank[pb_i][:].rearrange("p (c t) -> p c t", c=4)),
                           reads=[b_pb[pb_i]], writes=[bst])
                    else:
                        op("dve", lambda e, st=st, half=half, pb_i=pb_i, tb=tb: e.tensor_copy(
                            out=st[:, half * 4:half * 4 + 4, (tb % 4) * 128:(tb % 4 + 1) * 128],
                            in_=pbank[pb_i][:].rearrange("p (c t) -> p c t", c=4)),
                           reads=[b_pb[pb_i]], writes=[bst])
                if tb % 4 == 3:
                    t0 = (tb // 4) * 512
                    dma("pool", xT[:, :, t0:t0 + 512].rearrange("c p t -> p c t"), st[:], reads=[bst], writes=b_xTt[tb // 4])

            pi = 0
            xb_next = 0
            for l in range(L + 1):
                src = ada_w[l] if l < L else fada_w
                ncols = 6 * D if l < L else 2 * D
                pm = pbank[l % 2]
                bpm = b_pb[l % 2]
                nchunks = ncols // 128
                for pc in range((ncols + 767) // 768):
                    c0 = pc * 768
                    cw = min(768, ncols - c0)
                    w_ = wb[pi % 2]; bw = b_wb[pi % 2]
                    ws_ = wst[pi % 2]; bws = b_wst[pi % 2]; pi += 1
                    for kc in range(NCH):
                        dma("sp", ws_[:, kc, 0:cw], src[kc * 128:(kc + 1) * 128, c0:c0 + cw], writes=[bws], par=(kc > 0))
                    op("act", lambda e, w_=w_, ws_=ws_, cw=cw: e.copy(out=w_[:, 0:4, 0:cw], in_=ws_[:, 0:4, 0:cw]), reads=[bws], writes=[bw])
                    op("dve", lambda e, w_=w_, ws_=ws_, cw=cw: e.tensor_copy(out=w_[:, 4:8, 0:cw], in_=ws_[:, 4:8, 0:cw]), reads=[bws], writes=[bw])
                    for ncx in range(cw // 128):
                        n_abs = c0 // 128 + ncx
                        for kc in range(NCH):
                            op("pe", lambda e, w_=w_, kc=kc, ncx=ncx, n_abs=n_abs, pm=pm: e.matmul(
                                pm[:, n_abs * 2:n_abs * 2 + 2], lhsT=w_[:, kc, ncx * 128:(ncx + 1) * 128],
                                rhs=csT[:, kc, :], start=(kc == 0), stop=(kc == NCH - 1)),
                               reads=[bw, b_csT], writes=[bpm])
                    if xb_next < NB:
                        emit_xblock(xb_next); xb_next += 1
                op("dve", lambda e, l=l, pm=pm, nchunks=nchunks: e.tensor_tensor(
                    out=mods[:, l, 0:nchunks, :], in0=pm[:, 0:nchunks * 2].rearrange("p (n s) -> p n s", s=2),
                    in1=abt[:, l, 0:nchunks].unsqueeze(2).to_broadcast([128, nchunks, 2]), op=ALU.add),
                   reads=[bpm, b_abt], writes=[b_mods])
                if l < L:
                    for (dst, scidx, gi_) in ((48, 8, 0), (56, 32, 1)):
                        op("dve", lambda e, l=l, dst=dst, scidx=scidx, gi_=gi_: e.scalar_tensor_tensor(
                            out=mods[:, l, dst:dst + 8, :], in0=mods[:, l, scidx:scidx + 8, :], scalar=1.0,
                            in1=gT[:, l, gi_, :].unsqueeze(2).to_broadcast([128, NCH, 2]), op0=ALU.add, op1=ALU.mult),
                           reads=[b_gT], writes=[b_mods])
                    if layer_types[l] == "f":
                        fi = sum(1 for t in layer_types[:l] if t == "f")
                        op("dve", lambda e, l=l, fi=fi: e.tensor_tensor(
                            out=mods[:, l, 64:72, :], in0=mods[:, l, 16:24, :],
                            in1=fbT[:, fi, :].unsqueeze(2).to_broadcast([128, NCH, 2]), op=ALU.mult),
                           reads=[b_fbT], writes=[b_mods])
                else:
                    op("dve", lambda e, l=l: e.scalar_tensor_tensor(
                        out=mods[:, l, 48:56, :], in0=mods[:, l, 8:16, :], scalar=1.0,
                        in1=gT[:, l, 0, :].unsqueeze(2).to_broadcast([128, NCH, 2]), op0=ALU.add, op1=ALU.mult),
                       reads=[b_gT], writes=[b_mods])
            while xb_next < NB:
                emit_xblock(xb_next); xb_next += 1
            k.end_phase(pbufs)

        def mk_ring(stack, pbuf, pre, with_tmp=True, nsq=2):
            R = {}
            R["sq"] = [(sb(stack, f"{pre}sq{i}", [128, 512], BF16), pbuf(f"{pre}sq{i}")) for i in range(nsq)]
            if with_tmp:
                R["tmp"] = [(sb(stack, f"{pre}tmp{i}", [128, 512]), pbuf(f"{pre}tmp{i}")) for i in range(2)]
            R["rs"] = (sb(stack, f"{pre}rs", [128, 512]), pbuf(f"{pre}rs"))
            return R

        def norm_mod(xt, bxt, ht, bht, R, l, aidx, shidx, slot, ps_i):
            for _ in norm_mod_gen(xt, bxt, ht, bht, R, l, aidx, shidx, slot, ps_i):
                pass

        def norm_mod_gen(xt, bxt, ht, bht, R, l, aidx, shidx, slot, ps_i, inplace=False):
            nsq = len(R["sq"])
            if nsq >= NCH:
                for c in range(NCH):
                    sq, bsq = R["sq"][c]
                    op("act", lambda e, sq=sq, c=c: e.activation(out=sq[:], in_=xt[:, c, :], func=AF.Square), reads=[bxt], writes=[bsq])
                    if c % 2 == 1:
                        yield
                yield
                for c in range(NCH):
                    sq, bsq = R["sq"][c]
                    op("pe", lambda e, sq=sq, c=c: e.matmul(pbank[ps_i][:, :], lhsT=ones_b, rhs=sq[:],
                                                            start=(c == 0), stop=(c == NCH - 1)),
                       reads=[bsq, b_cstb], writes=[b_pb[ps_i]])
                yield
            else:
                for c in range(NCH):
                    sq, bsq = R["sq"][c % nsq]
                    op("act", lambda e, sq=sq, c=c: e.activation(out=sq[:], in_=xt[:, c, :], func=AF.Square), reads=[bxt], writes=[bsq])
                    op("pe", lambda e, sq=sq, c=c: e.matmul(pbank[ps_i][:, :], lhsT=ones_b, rhs=sq[:],
                                                            start=(c == 0), stop=(c == NCH - 1)),
                       reads=[bsq, b_cstb], writes=[b_pb[ps_i]])
                    yield
            rs, brs = R["rs"]
            op("act", lambda e: e.activation(out=rs[:], in_=pbank[ps_i][:, :], func=AF.Sqrt, bias=epsc[:, 0:1], scale=1.0 / D),
               reads=[b_pb[ps_i], b_eps], writes=[brs])
            op("dve", lambda e: e.reciprocal(out=rs[:], in_=rs[:]), reads=[], writes=[brs])
            yield
            for c in range(NCH):
                if inplace:
                    tmp, btmp = xt[:, c, :], bxt
                    op("dve", lambda e, tmp=tmp, c=c: e.tensor_tensor(out=tmp, in0=xt[:, c, :], in1=rs[:], op=ALU.mult),
                       reads=[brs], writes=[bxt])
                else:
                    tmp, btmp = R["tmp"][c % 2]
                    tmp = tmp[:]
                    op("dve", lambda e, tmp=tmp, c=c: e.tensor_tensor(out=tmp, in0=xt[:, c, :], in1=rs[:], op=ALU.mult),
                       reads=[bxt, brs], writes=[btmp])
                op("act", lambda e, tmp=tmp, c=c: e.activation(out=ht[:, c, :], in_=tmp, func=AF.Identity,
                                                               scale=mods[:, l, aidx + c, slot:slot + 1],
                                                               bias=mods[:, l, shidx + c, slot:slot + 1]),
                   reads=[btmp, b_mods], writes=[bht])
                yield

        def gdn_layer(l, gi):
            NG = NT // 512
            g1idx = 16
            with contextlib.ExitStack() as ph:
                pbufs = []
                def pbuf(n):
                    b = k.buf(n); pbufs.append(b); return b
                win = sb(ph, "win", [128, NCH, GP], BF16); b_win = pbuf("win")
                for kc in range(NCH):
                    dma("pool", win[:, kc, :], gdn_w_in[gi, kc * 128:(kc + 1) * 128, :], writes=[b_win], par=(kc > 0))
                xt = sb(ph, "gxt", [128, NCH, 512]); bx = pbuf("gxt")
                hts = [sb(ph, f"ght{i}", [128, NCH, 512], BF16) for i in range(2)]; bhts = [pbuf(f"ght{i}") for i in range(2)]
                RG = mk_ring(ph, pbuf, "g", nsq=8)

                def ga_norm_gen(t):
                    dma("sp", xt[:], xT[:, :, t * 512:(t + 1) * 512].rearrange("c p t -> p c t"), reads=b_xTt[t], writes=[bx])
                    yield
                    yield from norm_mod_gen(xt, bx, hts[t % 2], bhts[t % 2], RG, l, 48, 0, (t * 512) // SL, 0)
                pst = [sb(ph, f"pst{i}", [128, 4, 512], F32R) for i in range(2)]; b_pst = [pbuf(f"pst{i}") for i in range(2)]
                zst = sb(ph, "zst", [128, 8, 512], BF16); b_zst = pbuf("zst")
                dtb = sb(ph, "dtb", [32, 1]); b_dtb = pbuf("dtb")
                nal = sb(ph, "nal", [128, 16]); b_nal = pbuf("nal")
                dma("sp", dtb[:], dtb_col[gi], writes=[b_dtb])
                dma("sp", nal[:], alog_row[gi:gi + 1, :].to_broadcast([128, 16]), writes=[b_nal])
                op("act", lambda e: e.activation(out=nal[:], in_=nal[:], func=AF.Exp), writes=[b_nal])
                op("dve", lambda e: e.tensor_scalar(out=nal[:], in0=nal[:], scalar1=-1.0, scalar2=None, op0=ALU.mult), writes=[b_nal])
                gE = sb(ph, "gE", [32, 512]); b_gE = pbuf("gE")
                gA = sb(ph, "gA", [32, 512]); b_gA = pbuf("gA")
                gX = sb(ph, "gX", [32, 512]); b_gX = pbuf("gX")
                gS = sb(ph, "gS", [32, 512]); b_gS = pbuf("gS")
                gbt = sb(ph, "gbt", [128, 4, 32]); b_gbt = pbuf("gbt")
                for _ in ga_norm_gen(0):
                    pass
                for t in range(NTT):
                    t0 = t * 512
                    ht = hts[t % 2]; bht = bhts[t % 2]
                    nxt = ga_norm_gen(t + 1) if t + 1 < NTT else iter(())
                    for c in range(33):
                        if c >= 4:
                            next(nxt, None)
                        pbi = 1 + c % 4
                        rows = 128 if c < 32 else 32
                        for kc in range(NCH):
                            op("pe", lambda e, c=c, kc=kc, pbi=pbi, rows=rows: e.matmul(
                                pbank[pbi][0:rows, :], lhsT=win[:, kc, c * 128:c * 128 + rows], rhs=ht[:, kc, :],
                                start=(kc == 0), stop=(kc == NCH - 1)), reads=[b_win, bht], writes=[b_pb[pbi]])
                        if c < 24:
                            ps_ = pst[(c // 4) % 2]; bps = b_pst[(c // 4) % 2]
                            if c % 2 == 0:
                                op("act", lambda e, ps_=ps_, c=c, pbi=pbi: e.copy(out=ps_[:, c % 4, :], in_=pbank[pbi][:, :]),
                                   reads=[b_pb[pbi]], writes=[bps])
                            else:
                                op("dve", lambda e, ps_=ps_, c=c, pbi=pbi: e.tensor_copy(out=ps_[:, c % 4, :], in_=pbank[pbi][:, :]),
                                   reads=[b_pb[pbi]], writes=[bps])
                            if c % 4 == 3:
                                c0 = c - 3
                                dma("pool", P_s[c0:c0 + 4, :, t0:t0 + 512].rearrange("c p t -> p c t"), ps_[:].bitcast(F32), reads=[bps], writes=[b_Ps[t]], par=True)
                        elif c < 32:
                            op("act", lambda e, c=c, pbi=pbi: e.activation(out=zst[:, c - 24, :], in_=pbank[pbi][:, :], func=AF.Silu),
                               reads=[b_pb[pbi]], writes=[b_zst])
                            if c == 31:
                                dma("pool", zT_s[:, :, t0:t0 + 512].rearrange("c p t -> p c t"), zst[:], reads=[b_zst], writes=[b_zs[t]])
                        else:
                            pg = pbank[pbi][0:32, :]
                            op("act", lambda e, pg=pg: e.activation(out=gE[:], in_=pg, func=AF.Identity, bias=dtb[:, 0:1], scale=1.0),
                               reads=[b_pb[pbi], b_dtb], writes=[b_gE])
                            op("act", lambda e, pg=pg: e.activation(out=gS[:], in_=pg, func=AF.Sigmoid), reads=[b_pb[pbi]], writes=[b_gS])
                            op("act", lambda e: e.activation(out=gA[:], in_=gE[:], func=AF.Abs), reads=[b_gE], writes=[b_gA])
                            op("act", lambda e: e.activation(out=gA[:], in_=gA[:], func=AF.Exp, scale=-1.0), writes=[b_gA])
                            op("act", lambda e: e.activation(out=gA[:], in_=gA[:], func=AF.Ln, bias=1.0, scale=1.0), writes=[b_gA])
                            op("dve", lambda e: e.tensor_scalar(out=gX[:], in0=gE[:], scalar1=0.0, scalar2=None, op0=ALU.max), reads=[b_gE], writes=[b_gX])
                            op("dve", lambda e: e.tensor_tensor(out=gX[:], in0=gX[:], in1=gA[:], op=ALU.add), reads=[b_gA], writes=[b_gX])
                            gp = 1 + (c + 1) % 4
                            for s_ in range(4):
                                op("pe", lambda e, s_=s_, gp=gp: e.transpose(out=pbank[gp][:, s_ * 64:s_ * 64 + 32], in_=gX[:, s_ * 128:(s_ + 1) * 128],
                                                                          identity=ident_f[0:32, 0:32]), reads=[b_gX, b_cst], writes=[b_pb[gp]])
                                op("pe", lambda e, s_=s_, gp=gp: e.transpose(out=pbank[gp][:, s_ * 64 + 32:s_ * 64 + 64], in_=gS[:, s_ * 128:(s_ + 1) * 128],
                                                                          identity=ident_f[0:32, 0:32]), reads=[b_gS, b_cst], writes=[b_pb[gp]])
                            pv = pbank[gp][:, 0:256].rearrange("p (s x) -> p s x", s=4)
                            op("dve", lambda e, pv=pv: e.tensor_tensor(out=gbt[:, :, 0:16], in0=pv[:, :, 0:16],
                                                                       in1=nal[:].unsqueeze(1).to_broadcast([128, 4, 16]), op=ALU.mult),
                               reads=[b_pb[gp], b_nal], writes=[b_gbt])
                            op("act", lambda e, pv=pv: e.copy(out=gbt[:, :, 16:32], in_=pv[:, :, 48:64]), reads=[b_pb[gp]], writes=[b_gbt])
                            dma("pool", gb_s[t0:t0 + 512, :].rearrange("(s p) x -> p s x", p=128), gbt[:], reads=[b_gbt], writes=[b_gbs[t]])
                    for _ in nxt:
                        pass
                k.end_phase(pbufs)
            with contextlib.ExitStack() as ph:
                pbufs = []
                def pbuf(n):
                    b = k.buf(n); pbufs.append(b); return b
                cw = sb(ph, "cw", [128, 24, 5]); b_cw = pbuf("cw")
                dma("sp", cw[:], conv_wT[gi], writes=[b_cw])
                dg = sb(ph, "dg", [128, 24, 5, 128], F32R); b_dg = pbuf("dg")
                for c in range(24):
                    op("dve" if c % 2 == 0 else "pool", lambda e, c=c: e.tensor_tensor(
                        out=dg[:, c, :, :], in0=ident_f.unsqueeze(1).to_broadcast([128, 5, 128]),
                        in1=cw[:, c, :].unsqueeze(2).to_broadcast([128, 5, 128]), op=ALU.mult), reads=[b_cw, b_cst], writes=[b_dg])
                Ph = sb(ph, "Ph", [128, 24, 516], F32R); b_Php = [pbuf(f"Ph{i}") for i in range(3)]
                sa = sb(ph, "sa", [128, 16, 512]); b_sa = [pbuf(f"sa{i}") for i in range(16)]
                sq2 = [sb(ph, f"sqq{i}", [128, 512], BF16) for i in range(4)]; b_sq2 = [pbuf(f"sqq{i}") for i in range(4)]
                rq = [sb(ph, f"rq{i}", [128, 512]) for i in range(4)]; b_rq = [pbuf(f"rq{i}") for i in range(4)]
                qkv = sb(ph, "qkvT", [128, 24, 512], BF16); b_qkv = [pbuf(f"qkv{i}") for i in range(3)]
                tm = [sb(ph, f"tm{i}", [128, D], BF16) for i in range(2)]; b_tm = [pbuf(f"tm{i}") for i in range(2)]
                tmi = 0

                def load_ph(t, part):
                    t0 = t * 512
                    cs3 = slice(part * 8, part * 8 + 8)
                    bP = b_Php[part]
                    dma("sp", Ph[:, cs3, 2:514].bitcast(F32), P_s[cs3, :, t0:t0 + 512].rearrange("c p t -> p c t"), reads=[b_Ps[t]], writes=[bP])
                    with nc.allow_non_contiguous_dma(reason="conv halo"):
                        if t > 0:
                            dma("sp", Ph[:, cs3, 0:2].bitcast(F32), P_s[cs3, :, t0 - 2:t0].rearrange("c p t -> p c t"), reads=[b_Ps[t - 1]], writes=[bP], par=True)
                        if t < NTT - 1:
                            dma("sp", Ph[:, cs3, 514:516].bitcast(F32), P_s[cs3, :, t0 + 512:t0 + 514].rearrange("c p t -> p c t"), reads=[b_Ps[t + 1]], writes=[bP], par=True)
                    if t == 0:
                        op("pool", lambda e: e.memset(Ph[:, cs3, 0:2].bitcast(F32), 0.0), writes=[bP])
                    if t == NTT - 1:
                        op("pool", lambda e: e.memset(Ph[:, cs3, 514:516].bitcast(F32), 0.0), writes=[bP])
                    if t0 == SL:
                        op("pool", lambda e: e.tensor_scalar(out=Ph[:, cs3, 0:2], in0=Ph[:, cs3, 0:2], scalar1=flg[:, 0:1], scalar2=None, op0=ALU.mult),
                           reads=[b_flg], writes=[bP])
                    if t0 + 512 == SL:
                        op("pool", lambda e: e.tensor_scalar(out=Ph[:, cs3, 514:516], in0=Ph[:, cs3, 514:516], scalar1=flg[:, 0:1], scalar2=None, op0=ALU.mult),
                           reads=[b_flg], writes=[bP])

                for part in range(3):
                    load_ph(0, part)
                for t in range(NTT):
                    t0 = t * 512
                    for c in range(24):
                        pc = 1 + c % 4
                        for j in range(5):
                            op("pe", lambda e, c=c, j=j, pc=pc: e.matmul(pbank[pc][:, :], lhsT=dg[:, c, j, :], rhs=Ph[:, c, j:j + 512],
                                                                        start=(j == 0), stop=(j == 4)), reads=[b_dg, b_Php[c // 8]], writes=[b_pb[pc]])
                        if c < 16:
                            op("act", lambda e, c=c, pc=pc: e.activation(out=sa[:, c, :], in_=pbank[pc][:, :], func=AF.Silu), reads=[b_pb[pc]], writes=[b_sa[c]])
                        else:
                            op("act", lambda e, c=c, pc=pc: e.activation(out=qkv[:, c, :], in_=pbank[pc][:, :], func=AF.Silu), reads=[b_pb[pc]], writes=[b_qkv[2]])
                        if c % 8 == 7 and t + 1 < NTT:
                            load_ph(t + 1, c // 8)
                    PN = [5, 6, 7, 0]
                    def st_sq(c):
                        s2 = sq2[c % 4]; bs2 = b_sq2[c % 4]
                        op("dve", lambda e: e.tensor_tensor(out=s2[:], in0=sa[:, c, :], in1=sa[:, c, :], op=ALU.mult), reads=[b_sa[c]], writes=[bs2])
                    def st_mm(c):
                        s2 = sq2[c % 4]; bs2 = b_sq2[c % 4]; pn = PN[c % 4]
                        op("pe", lambda e: e.matmul(pbank[pn][:, :], lhsT=ones_b, rhs=s2[:], start=True, stop=True), reads=[bs2, b_cstb], writes=[b_pb[pn]])
                    def st_ln(c):
                        r_ = rq[c % 4]; br = b_rq[c % 4]; pn = PN[c % 4]
                        op("act", lambda e: e.activation(out=r_[:], in_=pbank[pn][:, :], func=AF.Ln, bias=epsc[:, 0:1], scale=1.0),
                           reads=[b_pb[pn], b_eps], writes=[br])
                    def st_ex(c):
                        r_ = rq[c % 4]; br = b_rq[c % 4]
                        op("act", lambda e: e.activation(out=r_[:], in_=r_[:], func=AF.Exp, scale=-0.5), writes=[br])
                    def st_out(c):
                        r_ = rq[c % 4]; br = b_rq[c % 4]
                        scl = (128.0 ** -0.5) if c < 8 else 1.0
                        op("dve", lambda e: e.scalar_tensor_tensor(out=qkv[:, c, :], in0=sa[:, c, :], scalar=scl, in1=r_[:], op0=ALU.mult, op1=ALU.mult),
                           reads=[b_sa[c], br], writes=[b_qkv[c // 8]])
                    stages = [st_sq, st_mm, st_ln, st_ex, st_out]
                    for i in range(16 + len(stages) - 1):
                        for si in range(len(stages) - 1, -1, -1):
                            c = i - si
                            if 0 <= c < 16:
                                stages[si](c)
                    dma("pool", qT_s[:, :, t0:t0 + 512].rearrange("c p t -> p c t"), qkv[:, 0:8, :], reads=[b_qkv[0]], writes=[b_qs[t]])
                    dma("pool", kT_s[:, :, t0:t0 + 512].rearrange("c p t -> p c t"), qkv[:, 8:16, :], reads=[b_qkv[1]], writes=[b_ks[t]])
                    for (base, dst, bdst, bsrc) in ((8, ktm_s, b_ktm, b_qkv[1]), (16, vtm_s, b_vtm, b_qkv[2])):
                        for s_ in range(4):
                            tpi = 1 + tmi % 4
                            tp = pbank[tpi][:].bitcast(BF16)
                            for h in range(8):
                                op("pe", lambda e, h=h, s_=s_, base=base, tp=tp: e.transpose(
                                    out=tp[:, h * 128:(h + 1) * 128], in_=qkv[:, base + h, s_ * 128:(s_ + 1) * 128], identity=ident_b),
                                   reads=[bsrc, b_cstb], writes=[b_pb[tpi]])
                            tm_ = tm[tmi % 2]; btm = b_tm[tmi % 2]
                            if tmi % 2 == 0:
                                op("act", lambda e, tm_=tm_, tp=tp: e.copy(out=tm_[:], in_=tp), reads=[b_pb[tpi]], writes=[btm])
                            else:
                                op("dve", lambda e, tm_=tm_, tp=tp: e.tensor_copy(out=tm_[:], in_=tp), reads=[b_pb[tpi]], writes=[btm])
                            tmi += 1
                            dma("pool", dst[t0 + s_ * 128:t0 + (s_ + 1) * 128, :], tm_[:], reads=[btm], writes=[bdst[t]], par=True)
                k.end_phase(pbufs)
            with contextlib.ExitStack() as ph:
                pbufs = []
                def pbuf(n):
                    b = k.buf(n); pbufs.append(b); return b
                def T_(name, shape, dt=F32):
                    return sb(ph, name, shape, dt), pbuf(name)
                GC = 4
                NGR = NT // (GC * 64)
                NCHK = NT // 64
                R = 4
                NPAR = int(os.environ.get("KDBG_NPAR", "4"))
                wout, b_wout = T_("wout", [128, 8, D], BF16)
                for kc in range(NCH):
                    dma("pool", wout[:, kc, :], gdn_w_out[gi, kc * 128:(kc + 1) * 128, :], writes=[b_wout], par=(kc > 0))
                gng, b_gng = T_("gng", [128, 1])
                dma("sp", gng[:], gng_col[gi], writes=[b_gng])
                GW = GC * 64
                grp = []
                for i in range(2):
                    grp.append(dict(
                        q=T_(f"qTg{i}", [128, 8, GW], BF16), k=T_(f"kTg{i}", [128, 8, GW], BF16),
                        ktm=T_(f"ktmg{i}", [64, GC, D], BF16), vtm=T_(f"vtmg{i}", [64, GC, D], BF16),
                        gb=T_(f"gbg{i}", [64, GC, 32]), z=T_(f"zTg{i}", [128, 8, GW], BF16)))
                ring = []
                for i in range(R):
                    ring.append(dict(Y=T_(f"rY{i}", [64, 8, 64], BF16), QKd=T_(f"rQKd{i}", [64, 8, 64], BF16),
                                     qg=T_(f"rqg{i}", [128, 8, 64], BF16), kd=T_(f"rkd{i}", [64, 8, 128], BF16),
                                     sm=T_(f"rsm{i}", [64, 24]), gl=T_(f"rgl{i}", [128, 8])))
                tmps = []
                for i in range(NPAR):
                    ga_ = T_(f"Gm{i}", [64, 8, 64])
                    tmps.append(dict(Gm=ga_, arg=ga_, Dm=T_(f"Dm{i}", [64, 8, 64]),
                                     egr=T_(f"egr{i}", [128, 8, 64]), t1=T_(f"t1{i}", [64, 8, 64]),
                                     NTb=T_(f"NTb{i}", [64, 8, 64], BF16), Nb=T_(f"Nb{i}", [64, 8, 64], BF16),
                                     Y=[T_(f"Y{i}_{j}", [64, 8, 64], BF16) for j in range(2)],
                                     P=[T_(f"P{i}_{j}", [64, 8, 64], BF16) for j in range(2)],
                                     Pt=[T_(f"Pt{i}_{j}", [64, 8, 64], BF16) for j in range(2)]))
                S, b_S = T_("S", [128, 8, 128]); Sb, b_Sb = T_("Sb", [128, 8, 128], BF16)
                tk, b_tk = T_("tk", [64, 8, 128])
                rr, b_rr = T_("rr", [64, 8, 128], BF16)
                dl, b_dl = T_("dl", [64, 8, 128], BF16)
                ost = [T_(f"ost{i}", [64, D]) for i in range(1)]
                ot, b_ot = T_("ot", [64, 8, 128])
                sqo, b_sqo = tk, b_tk
                ss, b_ss = T_("ss", [64, 8])
                on, b_on = T_("on", [64, 8, 128], BF16)
                ogT, b_ogT = T_("ogT", [128, 8, 512], BF16)
                xp = [T_(f"gxp{i}", [128, 512]) for i in range(2)]
                xpi = [0]
                ID64 = cst[0:64, C_ID, 0:64]
                IDB64 = cstb[0:64, C_ID, 0:64]
                def bc_h(ap2d):
                    return ap2d.unsqueeze(1).to_broadcast([64, 8, 64])

                def v3(ap, h=8):
                    return ap.rearrange("p (h i) -> p h i", h=h)

                def load_group(d, g):
                    G = grp[g % 2]
                    t0 = g * GW
                    tl = t0 // 512
                    dma("sp", G["q"][0][:], qT_s[:, :, t0:t0 + GW].rearrange("c p t -> p c t"), reads=[b_qs[tl]], writes=[G["q"][1]])
                    dma("sp", G["k"][0][:], kT_s[:, :, t0:t0 + GW].rearrange("c p t -> p c t"), reads=[b_ks[tl]], writes=[G["k"][1]])
                    dma("sp", G["ktm"][0][:], ktm_s[t0:t0 + GW, :].rearrange("(c p) x -> p c x", p=64), reads=[b_ktm[tl]], writes=[G["ktm"][1]])
                    dma("sp", G["vtm"][0][:], vtm_s[t0:t0 + GW, :].rearrange("(c p) x -> p c x", p=64), reads=[b_vtm[tl]], writes=[G["vtm"][1]])
                    dma("sp", G["gb"][0][:], gb_s[t0:t0 + GW, :].rearrange("(c p) x -> p c x", p=64), reads=[b_gbs[tl]], writes=[G["gb"][1]])
                    if d == 1:
                        dma("sp", G["z"][0][:], zT_s[:, :, t0:t0 + GW].rearrange("c p t -> p c t"), reads=[b_zs[tl]], writes=[G["z"][1]])

                def c1_gen(d, c, chain):
                    MC, NM, STR, REV = ((C_MCF, C_NMF, C_M2B, C_M2F) if d == 0 else (C_MCB, C_NMB, C_M2F, C_M2B))
                    mc = cst[0:64, MC, 0:64]; nm = cst[0:64, NM, 0:64]; st_ = cst[0:64, STR, 0:64]; rv = cst[0:64, REV, 0:64]
                    g = c // GC; ci = c % GC
                    G = grp[g % 2]
                    qTg, b_qTg = G["q"]; kTg, b_kTg = G["k"]; ktm, b_ktmg = G["ktm"]; gbg, b_gbg = G["gb"]
                    cs_ = slice(ci * 64, ci * 64 + 64)
                    Tm = tmps[chain]; Rg = ring[c % R]
                    pb = pbank[chain]; bpb = b_pb[chain]
                    Gm, b_Gm = Tm["Gm"]; arg, b_arg = Tm["arg"]; Dm, b_Dm = Tm["Dm"]; egr, b_egr = Tm["egr"]; t1, b_t1 = Tm["t1"]
                    NTb, b_NTb = Tm["NTb"]; Nb, b_Nb = Tm["Nb"]
                    sm, b_sm = Rg["sm"]; gl, b_gl = Rg["gl"]; qg, b_qg = Rg["qg"]; kd, b_kd = Rg["kd"]; QKd, b_QKd = Rg["QKd"]
                    g_c = gbg[:, ci, d * 8:(d + 1) * 8]
                    be_c = gbg[:, ci, 16 + d * 8:16 + (d + 1) * 8]
                    op("dve", lambda e: e.tensor_tensor(out=Gm[:], in0=bc_h(mc), in1=g_c.unsqueeze(2).to_broadcast([64, 8, 64]), op=ALU.mult),
                       reads=[b_gbg, b_cst], writes=[b_Gm])
                    yield
                    op("pe", lambda e: e.matmul(pb[0:64, 0:8], lhsT=mc, rhs=g_c, start=True, stop=True), reads=[b_gbg, b_cst], writes=[bpb])
                    op("pe", lambda e: e.matmul(pb[0:64, 8:16], lhsT=rv, rhs=g_c, start=True, stop=True), reads=[b_gbg, b_cst], writes=[bpb])
                    op("pe", lambda e: e.matmul(pb[:, 16:24], lhsT=cst[0:64, C_ONES, :], rhs=g_c, start=True, stop=True), reads=[b_gbg, b_cst], writes=[bpb])
                    yield
                    op("act", lambda e: e.activation(out=sm[:, 0:16], in_=pb[0:64, 0:16], func=AF.Exp), reads=[bpb], writes=[b_sm])
                    op("act", lambda e: e.activation(out=gl[:], in_=pb[:, 16:24], func=AF.Exp), reads=[bpb], writes=[b_gl])
                    op("act", lambda e: e.copy(out=sm[:, 16:24], in_=pb[0:64, 0:8]), reads=[bpb], writes=[b_sm])
                    yield
                    op("pool", lambda e: e.tensor_tensor(out=kd[:], in0=ktm[:, ci, :].rearrange("p (h v) -> p h v", h=8),
                                                         in1=sm[:, 8:16].unsqueeze(2).to_broadcast([64, 8, 128]), op=ALU.mult),
                       reads=[b_ktmg, b_sm], writes=[b_kd])
                    op("pe", lambda e: e.matmul(pb[:, :], lhsT=cst[0:64, C_ONES, :], rhs=Gm[:].rearrange("p h i -> p (h i)"), start=True, stop=True),
                       reads=[b_Gm, b_cst], writes=[bpb])
                    yield
                    op("dve", lambda e: e.tensor_tensor(out=arg[:], in0=v3(pb[0:64, :]), in1=bc_h(nm), op=ALU.add), reads=[bpb, b_cst], writes=[b_arg])
                    op("act", lambda e: e.activation(out=egr[:], in_=v3(pb[:, :]), func=AF.Exp), reads=[bpb], writes=[b_egr])
                    yield
                    op("pool", lambda e: e.tensor_tensor(out=arg[:], in0=arg[:], in1=sm[:, 16:24].unsqueeze(2).to_broadcast([64, 8, 64]), op=ALU.subtract),
                       reads=[b_sm], writes=[b_arg])
                    for h in range(8):
                        op("pe", lambda e, h=h: e.matmul(pb[0:64, h * 64:(h + 1) * 64], lhsT=kTg[:, h, cs_], rhs=kTg[:, h, cs_], start=True, stop=True),
                           reads=[b_kTg], writes=[bpb])
                    yield
                    op("act", lambda e: e.activation(out=Dm[:], in_=arg[:], func=AF.Exp), reads=[b_arg], writes=[b_Dm])
                    op("dve", lambda e: e.tensor_tensor(out=qg[:], in0=qTg[:, :, cs_], in1=egr[:], op=ALU.mult), reads=[b_qTg, b_egr], writes=[b_qg])
                    yield
                    op("dve", lambda e: e.tensor_tensor(out=t1[:], in0=v3(pb[0:64, :]), in1=Dm[:], op=ALU.mult), reads=[bpb, b_Dm], writes=[b_t1])
                    yield
                    for h in range(8):
                        op("pe", lambda e, h=h: e.matmul(pb[0:64, h * 64:(h + 1) * 64], lhsT=kTg[:, h, cs_], rhs=qTg[:, h, cs_], start=True, stop=True),
                           reads=[b_kTg, b_qTg], writes=[bpb])
                    op("pool", lambda e: e.tensor_tensor(out=t1[:], in0=t1[:], in1=bc_h(st_), op=ALU.mult), reads=[b_cst], writes=[b_t1])
                    yield
                    op("dve", lambda e: e.tensor_tensor(out=QKd[:], in0=v3(pb[0:64, :]), in1=Dm[:], op=ALU.mult), reads=[bpb, b_Dm], writes=[b_QKd])
                    op("dve", lambda e: e.tensor_tensor(out=t1[:], in0=t1[:], in1=be_c.unsqueeze(2).to_broadcast([64, 8, 64]), op=ALU.mult),
                       reads=[b_gbg], writes=[b_t1])
                    yield
                    op("act", lambda e: e.copy(out=NTb[:], in_=t1[:]), reads=[b_t1], writes=[b_NTb])
                    Y0, bY0 = Tm["Y"][0]
                    op("pool", lambda e: e.tensor_tensor(out=Y0[:], in0=bc_h(ID64), in1=t1[:], op=ALU.subtract), reads=[b_t1, b_cst], writes=[bY0])
                    yield
                    tp = pb[:].bitcast(BF16)
                    for h in range(8):
                        op("pe", lambda e, h=h: e.transpose(out=tp[0:64, h * 64:(h + 1) * 64], in_=NTb[:, h, :], identity=IDB64),
                           reads=[b_NTb, b_cstb], writes=[bpb])
                    yield
                    op("act", lambda e: e.copy(out=Nb[:], in_=v3(tp[0:64, 0:512])), reads=[bpb], writes=[b_Nb])
                    yield
                    Pc, bPc = Nb, b_Nb
                    Ptc, bPtc = NTb, b_NTb
                    Yc, bYc = Y0, bY0
                    for lev in range(5):
                        Pn, bPn = Tm["P"][lev % 2]; Ptn, bPtn = Tm["Pt"][lev % 2]
                        Yn, bYn = (Tm["Y"][(lev + 1) % 2] if lev < 4 else Rg["Y"])
                        for h in range(8):
                            op("pe", lambda e, h=h, Pc=Pc, Ptc=Ptc: e.matmul(pb[0:64, h * 64:(h + 1) * 64], lhsT=Ptc[:, h, :], rhs=Pc[:, h, :], start=True, stop=True),
                               reads=[bPc, bPtc], writes=[bpb])
                        yield
                        op("act", lambda e, Pn=Pn: e.copy(out=Pn[:], in_=v3(pb[0:64, :])), reads=[bpb], writes=[bPn])
                        yield
                        if lev < 4:
                            for h in range(8):
                                op("pe", lambda e, h=h, Pc=Pc, Ptc=Ptc: e.matmul(pb[0:64, h * 64:(h + 1) * 64], lhsT=Pc[:, h, :], rhs=Ptc[:, h, :], start=True, stop=True),
                                   reads=[bPc, bPtc], writes=[bpb])
                            yield
                            op("act", lambda e, Ptn=Ptn: e.copy(out=Ptn[:], in_=v3(pb[0:64, :])), reads=[bpb], writes=[bPtn])
                            yield
                        for h in range(8):
                            op("pe", lambda e, h=h, Pn=Pn, Yc=Yc: e.matmul(pb[0:64, h * 64:(h + 1) * 64], lhsT=Pn[:, h, :], rhs=Yc[:, h, :], start=True, stop=True),
                               reads=[bPn, bYc], writes=[bpb])
                        yield
                        op("dve", lambda e, Yn=Yn, Yc=Yc: e.tensor_tensor(out=Yn[:], in0=v3(pb[0:64, :]), in1=Yc[:], op=ALU.add), reads=[bpb, bYc], writes=[bYn])
                        yield
                        Pc, bPc, Ptc, bPtc, Yc, bYc = Pn, bPn, Ptn, bPtn, Yn, bYn

                def scan_gen(d, c):
                    g = c // GC; ci = c % GC
                    G = grp[g % 2]
                    kTg, b_kTg = G["k"]; vtm, b_vtmg = G["vtm"]; gbg, b_gbg = G["gb"]; zTg, b_zTg = G["z"]
                    cs_ = slice(ci * 64, ci * 64 + 64)
                    Rg = ring[c % R]
                    Yc, bYc = Rg["Y"]; QKd, b_QKd = Rg["QKd"]; qg, b_qg = Rg["qg"]; kd, b_kd = Rg["kd"]; sm, b_sm = Rg["sm"]; gl, b_gl = Rg["gl"]
                    be_c = gbg[:, ci, 16 + d * 8:16 + (d + 1) * 8]
                    tok0 = c * 64
                    if (d == 0 and tok0 == SL) or (d == 1 and tok0 == SL - 64):
                        op("pool", lambda e: e.tensor_scalar(out=S[:], in0=S[:], scalar1=flg[:, 0:1], scalar2=None, op0=ALU.mult),
                           reads=[b_flg], writes=[b_S])
                        op("act", lambda e: e.copy(out=Sb[:], in_=S[:]), reads=[b_S], writes=[b_Sb])
                    ost_, b_ost = ost[0]
                    if d == 1:
                        dma("act", ost_[:], o_s[tok0:tok0 + 64, :], reads=[b_os[tok0 // 512]], writes=[b_ost])
                    for h in range(8):
                        op("pe", lambda e, h=h: e.matmul(pbank[4 + h // 4][0:64, (h % 4) * 128:(h % 4 + 1) * 128], lhsT=kTg[:, h, cs_], rhs=Sb[:, h, :],
                                                         start=True, stop=True), reads=[b_kTg, b_Sb], writes=[b_pb[4 + h // 4]])
                    yield
                    for hh in range(2):
                        op("dve", lambda e, hh=hh: e.tensor_tensor(out=tk[:, hh * 4:hh * 4 + 4, :], in0=v3(pbank[4 + hh][0:64, :], 4),
                                                                   in1=sm[:, hh * 4:hh * 4 + 4].unsqueeze(2).to_broadcast([64, 4, 128]), op=ALU.mult),
                           reads=[b_pb[4 + hh], b_sm], writes=[b_tk])
                    yield
                    op("pool", lambda e: e.tensor_tensor(out=rr[:], in0=vtm[:, ci, :].rearrange("p (h v) -> p h v", h=8), in1=tk[:], op=ALU.subtract),
                       reads=[b_vtmg, b_tk], writes=[b_rr])
                    yield
                    for h in range(8):
                        op("pe", lambda e, h=h: e.matmul(pbank[4 + h // 4][0:64, (h % 4) * 128:(h % 4 + 1) * 128], lhsT=Yc[:, h, :], rhs=rr[:, h, :],
                                                         start=True, stop=True), reads=[bYc, b_rr], writes=[b_pb[4 + h // 4]])
                    yield
                    op("dve", lambda e: e.tensor_tensor(out=dl[:, 0:4, :], in0=v3(pbank[4][0:64, :], 4),
                                                        in1=be_c[:, 0:4].unsqueeze(2).to_broadcast([64, 4, 128]), op=ALU.mult),
                       reads=[b_pb[4], b_gbg], writes=[b_dl])
                    op("pool", lambda e: e.tensor_scalar(out=S[:], in0=S[:], scalar1=1.0, scalar2=None, op0=ALU.mult) if False else
                       e.tensor_tensor(out=S[:], in0=S[:], in1=gl[:].unsqueeze(2).to_broadcast([128, 8, 128]), op=ALU.mult),
                       reads=[b_gl], writes=[b_S])
                    op("dve", lambda e: e.tensor_tensor(out=dl[:, 4:8, :], in0=v3(pbank[5][0:64, :], 4),
                                                        in1=be_c[:, 4:8].unsqueeze(2).to_broadcast([64, 4, 128]), op=ALU.mult),
                       reads=[b_pb[5], b_gbg], writes=[b_dl])
                    yield
                    for h in range(8):
                        op("pe", lambda e, h=h: e.matmul(pbank[4 + h // 4][:, (h % 4) * 128:(h % 4 + 1) * 128], lhsT=kd[:, h, :], rhs=dl[:, h, :],
                                                         start=True, stop=True), reads=[b_kd, b_dl], writes=[b_pb[4 + h // 4]])
                    for h in range(8):
                        pbi = 6 + h // 4
                        osl = pbank[pbi][0:64, (h % 4) * 128:(h % 4 + 1) * 128]
                        op("pe", lambda e, h=h, osl=osl: e.matmul(osl, lhsT=qg[:, h, :], rhs=Sb[:, h, :], start=True, stop=False),
                           reads=[b_qg, b_Sb], writes=[b_pb[pbi]])
                        op("pe", lambda e, h=h, osl=osl: e.matmul(osl, lhsT=QKd[:, h, :], rhs=dl[:, h, :], start=False, stop=True),
                           reads=[b_QKd, b_dl], writes=[b_pb[pbi]])
                    yield
                    for hh in range(2):
                        op("dve", lambda e, hh=hh: e.tensor_tensor(out=S[:, hh * 4:hh * 4 + 4, :], in0=S[:, hh * 4:hh * 4 + 4, :],
                                                                   in1=v3(pbank[4 + hh][:, :], 4), op=ALU.add),
                           reads=[b_pb[4 + hh]], writes=[b_S])
                    yield
                    op("act", lambda e: e.copy(out=Sb[:], in_=S[:]), reads=[b_S], writes=[b_Sb])
                    if d == 0:
                        for hh in range(2):
                            op("act", lambda e, hh=hh: e.copy(out=ost_[:, hh * 512:(hh + 1) * 512], in_=pbank[6 + hh][0:64, :]),
                               reads=[b_pb[6 + hh]], writes=[b_ost])
                        dma("act", o_s[tok0:tok0 + 64, :], ost_[:], reads=[b_ost], writes=[b_os[tok0 // 512]], par=True)
                        yield
                    else:
                        for hh in range(2):
                            op("dve", lambda e, hh=hh: e.tensor_tensor(out=ot[:, hh * 4:hh * 4 + 4, :], in0=v3(pbank[6 + hh][0:64, :], 4),
                                                                       in1=v3(ost_[:, hh * 512:(hh + 1) * 512], 4), op=ALU.add),
                               reads=[b_pb[6 + hh], b_ost], writes=[b_ot])
                        yield
                        op("pool", lambda e: e.tensor_tensor(out=sqo[:], in0=ot[:], in1=ot[:], op=ALU.mult), reads=[b_ot], writes=[b_sqo])
                        yield
                        op("dve", lambda e: e.tensor_reduce(out=ss[:], in_=sqo[:], axis=AX.X, op=ALU.add), reads=[b_sqo], writes=[b_ss])
                        yield
                        op("act", lambda e: e.activation(out=ss[:], in_=ss[:], func=AF.Ln, bias=epsc[0:64, 0:1], scale=1.0 / 128.0),
                           reads=[b_eps], writes=[b_ss])
                        op("act", lambda e: e.activation(out=ss[:], in_=ss[:], func=AF.Exp, scale=-0.5), writes=[b_ss])
                        yield
                        op("dve", lambda e: e.tensor_tensor(out=on[:], in0=ot[:], in1=ss[:].unsqueeze(2).to_broadcast([64, 8, 128]), op=ALU.mult),
                           reads=[b_ot, b_ss], writes=[b_on])
                        yield
                        tpo = pbank[6][:].bitcast(BF16)
                        for h in range(8):
                            op("pe", lambda e, h=h: e.transpose(out=tpo[:, h * 64:(h + 1) * 64], in_=on[:, h, :], identity=IDB64),
                               reads=[b_on, b_cstb], writes=[b_pb[6]])
                        yield
                        c8 = (tok0 % 512) // 64
                        op("dve", lambda e: e.scalar_tensor_tensor(out=ogT[:, :, c8 * 64:(c8 + 1) * 64], in0=v3(tpo[:, 0:512]),
                                                                   scalar=gng[:, 0:1], in1=zTg[:, :, cs_], op0=ALU.mult, op1=ALU.mult),
                           reads=[b_pb[6], b_gng, b_zTg], writes=[b_ogT])
                        yield
                        if tok0 % 512 == 0:
                            t0 = tok0
                            tl = t0 // 512
                            slot = t0 // SL
                            for dc in range(NCH):
                                xp_, bxp = xp[xpi[0] % 2]; xpi[0] += 1
                                dma("sp", xp_[:], xT[dc, :, t0:t0 + 512], reads=[b_xTt[tl][dc]], writes=[bxp])
                                for h in range(8):
                                    op("pe", lambda e, h=h, dc=dc: e.matmul(pbank[7][:, :], lhsT=wout[:, h, dc * 128:(dc + 1) * 128], rhs=ogT[:, h, :],
                                                                           start=(h == 0), stop=(h == 7)), reads=[b_wout, b_ogT], writes=[b_pb[7]])
                                yield
                                op("dve", lambda e, dc=dc, xp_=xp_, slot=slot: e.scalar_tensor_tensor(
                                    out=xp_[:], in0=pbank[7][:, :], scalar=mods[:, l, g1idx + dc, slot:slot + 1], in1=xp_[:], op0=ALU.mult, op1=ALU.add),
                                   reads=[b_pb[7], b_mods], writes=[bxp])
                                dma("act", xT[dc, :, t0:t0 + 512], xp_[:], reads=[bxp], writes=[b_xTt[tl][dc]])
                                yield

                for d in range(int(os.environ.get("KDBG_ND", "2"))):
                    op("pool", lambda e: e.memset(S[:], 0.0), writes=[b_S])
                    op("pool", lambda e: e.memset(Sb[:], 0.0), writes=[b_Sb])
                    order = list(range(NCHK)) if d == 0 else list(range(NCHK - 1, -1, -1))
                    nxt_c1 = 0
                    c1_done = 0
                    scan_done = 0
                    active = {}
                    finished = set()
                    scan_g = None
                    loaded = set()
                    ycount = {}
                    while scan_done < NCHK:
                        for chain in range(NPAR):
                            if chain not in active and nxt_c1 < NCHK and nxt_c1 < scan_done + R:
                                c = order[nxt_c1]
                                g = c // GC
                                if g not in loaded:
                                    prev_needed = nxt_c1 - GC
                                    if scan_done < max(0, nxt_c1 - GC):
                                        continue
                                    load_group(d, g); loaded.add(g)
                                active[chain] = (nxt_c1, c1_gen(d, c, chain))
                                nxt_c1 += 1
                        for chain in list(active.keys()):
                            pos, gen_ = active[chain]
                            try:
                                next(gen_)
                                ycount[pos] = ycount.get(pos, 0) + 1
                                if ycount[pos] >= int(os.environ.get("KDBG_C1STOP", "1000")):
                                    raise StopIteration
                            except StopIteration:
                                finished.add(pos)
                                del active[chain]
                        while c1_done in finished:
                            c1_done += 1
                        if scan_g is None and scan_done < c1_done:
                            scan_g = scan_gen(d, order[scan_done]) if not os.environ.get("KDBG_NOSCAN") else iter(())
                        if scan_g is not None:
                            try:
                                next(scan_g)
                            except StopIteration:
                                scan_g = None
                                scan_done += 1
                k.end_phase(pbufs)

        fi_cnt = 0
        gi_cnt = 0
        for l in range(L):
            lt = layer_types[l]
            import os
            if os.environ.get("KDBG_SKIP_MIXER"):
                lt = "skip"
            if lt == "skip":
                pass
            elif lt == "f":
                fi = fi_cnt; fi_cnt += 1
                with contextlib.ExitStack() as ph:
                    pbufs = []
                    def pbuf(n):
                        b = k.buf(n); pbufs.append(b); return b
                    U = sb(ph, "U", [128, NB, D], BF16); V = sb(ph, "V", [128, NB, D], BF16)
                    b_U = [pbuf(f"U{i}") for i in range(NB)]
                    with contextlib.ExitStack() as ph1:
                        p1 = []
                        def pbuf1(n):
                            b = k.buf(n); p1.append(b); return b
                        xt = [sb(ph1, f"fxt{i}", [128, NCH, 512]) for i in range(2)]; bxt = [pbuf1(f"fxt{i}") for i in range(2)]
                        hts = [sb(ph1, f"fht{i}", [128, NCH, 512], BF16) for i in range(2)]; bhts = [pbuf1(f"fht{i}") for i in range(2)]
                        RG = mk_ring(ph1, pbuf1, "f", nsq=8)

                        def f_norm_gen(t):
                            x_ = xt[t % 2]; bx = bxt[t % 2]
                            dma("sp", x_[:], xT[:, :, t * 512:(t + 1) * 512].rearrange("c p t -> p c t"), reads=b_xTt[t], writes=[bx])
                            yield
                            yield from norm_mod_gen(x_, bx, hts[t % 2], bhts[t % 2], RG, l, 48, 0, (t * 512) // SL, 0)
                        cd = sb(ph1, "cd", [128, 256], BF16); bcd = pbuf1("cd")
                        dma("sp", cd[:], cdft, writes=[bcd])
                        for _ in f_norm_gen(0):
                            pass
                        for t in range(NTT):
                            ht = hts[t % 2]; bht = bhts[t % 2]
                            nxt = f_norm_gen(t + 1) if t + 1 < NTT else iter(())
                            for s in range(4):
                                tb = t * 4 + s
                                for cs_ in range(2):
                                    for half in range(2):
                                        next(nxt, None)
                                        next(nxt, None)
                                        pb_i = 1 + (s * 4 + cs_ * 2 + half) % 6
                                        for g4 in range(4):
                                            g = half * 4 + g4
                                            op("pe", lambda e, g=g, g4=g4, s=s, cs_=cs_, pb_i=pb_i: e.matmul(
                                                pbank[pb_i][:, g4 * 128:(g4 + 1) * 128], lhsT=ht[:, g, s * 128:(s + 1) * 128],
                                                rhs=cd[:, cs_ * 128:(cs_ + 1) * 128], start=True, stop=True),
                                               reads=[bht, bcd], writes=[b_pb[pb_i]])
                                        dst = (U if cs_ == 0 else V)
                                        if (cs_ + half) % 2 == 0:
                                            op("act", lambda e, dst=dst, tb=tb, half=half, pb_i=pb_i: e.copy(
                                                out=dst[:, tb, half * 512:(half + 1) * 512], in_=pbank[pb_i][:, :]),
                                               reads=[b_pb[pb_i]], writes=[b_U[tb]])
                                        else:
                                            op("dve", lambda e, dst=dst, tb=tb, half=half, pb_i=pb_i: e.tensor_copy(
                                                out=dst[:, tb, half * 512:(half + 1) * 512], in_=pbank[pb_i][:, :]),
                                               reads=[b_pb[pb_i]], writes=[b_U[tb]])
                            for _ in nxt:
                                pass
                        k.end_phase(p1)
                    fw = sb(ph, "fw", [128, NCH, D], BF16); bfw = pbuf("fw")
                    for kc in range(NCH):
                        dma("pool", fw[:, kc, :], fnet_w[fi, kc * 128:(kc + 1) * 128, :], writes=[bfw], par=(kc > 0))
                    NR = 3
                    mring = [sb(ph, f"mr{i}", [128, NB, 128], BF16) for i in range(NR)]
                    b_mr = [pbuf(f"mr{i}") for i in range(NR)]
                    mtm = [sb(ph, f"mtm{i}", [128, D], BF16) for i in range(2)]; b_mtm = [pbuf(f"mtm{i}") for i in range(2)]
                    mT = [sb(ph, f"mT{i}", [128, NCH, 512], BF16) for i in range(1)]; b_mT = [pbuf(f"mT{i}") for i in range(1)]
                    xp = [sb(ph, f"xp{i}", [128, 512]) for i in range(3)]; b_xp = [pbuf(f"xp{i}") for i in range(3)]
                    yt = [sb(ph, f"yt{i}", [128, 512]) for i in range(2)]; b_yt = [pbuf(f"yt{i}") for i in range(2)]
                    xpi = 0
                    xpi_l = [0]

                    def post(m):
                        mt = mtm[m % 2]; bmt = b_mtm[m % 2]
                        tpi = 5 + (m % 2)
                        tp = pbank[tpi][:].bitcast(BF16)
                        for c in range(NCH):
                            op("pe", lambda e, mt=mt, c=c, tp=tp: e.transpose(out=tp[:, c * 128:(c + 1) * 128],
                                                                              in_=mt[:, c * 128:(c + 1) * 128], identity=ident_b),
                               reads=[bmt, b_cstb], writes=[b_pb[tpi]])
                        g4 = m // 4
                        mT_ = mT[0]; bmT = b_mT[0]
                        op("act" if m % 2 == 0 else "dve",
                           (lambda e, mT_=mT_, tp=tp, m=m: e.copy(out=mT_[:, :, (m % 4) * 128:(m % 4 + 1) * 128],
                                                                 in_=tp.rearrange("p (c t) -> p c t", c=NCH))) if m % 2 == 0 else
                           (lambda e, mT_=mT_, tp=tp, m=m: e.tensor_copy(out=mT_[:, :, (m % 4) * 128:(m % 4 + 1) * 128],
                                                                        in_=tp.rearrange("p (c t) -> p c t", c=NCH))),
                           reads=[b_pb[tpi]], writes=[bmT])
                        if m % 4 == 3:
                            t0 = (m // 4) * 512
                            slot = t0 // SL
                            for dc in range(NCH):
                                xp_ = xp[xpi_l[0] % 3]; bxp = b_xp[xpi_l[0] % 3]; xpi_l[0] += 1
                                dma("pool", xp_[:], xT[dc, :, t0:t0 + 512], reads=[b_xTt[m // 4][dc]], writes=[bxp])
                                ypi = 7 if dc % 2 == 0 else 0
                                for c in range(NCH):
                                    op("pe", lambda e, c=c, dc=dc, mT_=mT_, ypi=ypi: e.matmul(
                                        pbank[ypi][:, :], lhsT=fw[:, c, dc * 128:(dc + 1) * 128], rhs=mT_[:, c, :],
                                        start=(c == 0), stop=(c == NCH - 1)), reads=[bfw, bmT], writes=[b_pb[ypi]])
                                y_ = yt[dc % 2]; by = b_yt[dc % 2]
                                op("act", lambda e, y_=y_, ypi=ypi, dc=dc, slot=slot: e.activation(
                                    out=y_[:], in_=pbank[ypi][:, :], func=AF.Identity,
                                    scale=mods[:, l, 16 + dc, slot:slot + 1], bias=mods[:, l, 64 + dc, slot:slot + 1]),
                                   reads=[b_pb[ypi], b_mods], writes=[by])
                                op("dve", lambda e, y_=y_, xp_=xp_: e.tensor_tensor(out=xp_[:], in0=xp_[:], in1=y_[:], op=ALU.add),
                                   reads=[by], writes=[bxp])
                                dma("pool", xT[dc, :, t0:t0 + 512], xp_[:], reads=[bxp], writes=[b_xTt[m // 4][dc]])

                    def dft_load(idx):
                        m_, which = idx // 2, idx % 2
                        if m_ < NB:
                            dma("sp", mring[idx % NR][:], (dftc if which == 0 else dfts)[m_], writes=[b_mr[idx % NR]])
                    for idx in range(NR):
                        dft_load(idx)
                    for m in range(NB):
                        pa, pb_ = 1 + (m % 2) * 2, 2 + (m % 2) * 2
                        for which, (src, first, last) in enumerate(((U, True, False), (V, False, True))):
                            idx = 2 * m + which
                            mat = mring[idx % NR]; bmat = b_mr[idx % NR]
                            for kk in range(NB):
                                for half in range(2):
                                    pbi = pa if half == 0 else pb_
                                    op("pe", lambda e, mat=mat, kk=kk, half=half, pbi=pbi, src=src, first=first, last=last: e.matmul(
                                        pbank[pbi][:, :], lhsT=mat[:, kk, :], rhs=src[:, kk, half * 512:(half + 1) * 512],
                                        start=(first and kk == 0), stop=(last and kk == NB - 1)),
                                       reads=[bmat, b_U[kk]], writes=[b_pb[pbi]])
                            dft_load(idx + NR)
                            if which == 0 and m >= 1:
                                post(m - 1)
                        mt = mtm[m % 2]; bmt = b_mtm[m % 2]
                        op("act", lambda e, mt=mt, pa=pa: e.copy(out=mt[:, 0:512], in_=pbank[pa][:, :]), reads=[b_pb[pa]], writes=[bmt])
                        op("dve", lambda e, mt=mt, pb_=pb_: e.tensor_copy(out=mt[:, 512:1024], in_=pbank[pb_][:, :]), reads=[b_pb[pb_]], writes=[bmt])
                    post(NB - 1)
                    k.end_phase(pbufs)
            else:
                gi = gi_cnt; gi_cnt += 1
                gdn_layer(l, gi)
            if os.environ.get("KDBG_SKIP_FFN"):
                continue
            with contextlib.ExitStack() as ph:
                pbufs = []
                def pbuf(n):
                    b = k.buf(n); pbufs.append(b); return b
                wgu = sb(ph, "wgu", [128, NCH, 2 * FF], BF16); b_wgu = [pbuf(f"wgu{i}") for i in range(11)]
                wd = sb(ph, "wd", [128, NFC, D], BF16); b_wd = [pbuf(f"wd{i}") for i in range(NFC)]
                order = [0, 5, 6, 1, 7, 2, 8, 3, 9, 4, 10]
                for blk in order:
                    for kc in range(NCH):
                        dma("pool", wgu[:, kc, blk * 512:(blk + 1) * 512],
                            ffn_w_gu[l, kc * 128:(kc + 1) * 128, blk * 512:(blk + 1) * 512], writes=[b_wgu[blk]], par=(kc > 0))
                for fc in range(NFC):
                    dma("pool", wd[:, fc, :], ffn_w_down[l, fc * 128:(fc + 1) * 128, :], writes=[b_wd[fc]])
                xt = sb(ph, "xt", [128, NCH, 512]); bx = pbuf("xt")
                xp = [sb(ph, f"xp{i}", [128, 512]) for i in range(2)]; b_xp = [pbuf(f"xp{i}") for i in range(2)]
                hts = [sb(ph, f"ht{i}", [128, NCH, 512], BF16) for i in range(2)]; bhts = [pbuf(f"ht{i}") for i in range(2)]
                RG = mk_ring(ph, pbuf, "n", with_tmp=False)

                def ffn_norm_gen(t):
                    dma("sp", xt[:], xT[:, :, t * 512:(t + 1) * 512].rearrange("c p t -> p c t"), reads=b_xTt[t], writes=[bx])
                    yield
                    yield from norm_mod_gen(xt, bx, hts[t % 2], bhts[t % 2], RG, l, 56, 24, (t * 512) // SL, 0, inplace=True)
                act = sb(ph, "act", [128, NFC, 512], BF16); b_act = [pbuf(f"act{i}") for i in range(NFC)]
                sg = [sb(ph, f"sg{i}", [128, 512]) for i in range(2)]; b_sg = [pbuf(f"sg{i}") for i in range(2)]
                xpi = 0
                for _ in ffn_norm_gen(0):
                    pass
                for t in range(NTT):
                    slot = (t * 512) // SL
                    ht = hts[t % 2]; bht = bhts[t % 2]
                    nxt = ffn_norm_gen(t + 1) if t + 1 < NTT else iter(())
                    for j in range(NFC):
                        if j >= 2:
                            next(nxt, None)
                        pg, pu = 1 + (j % 2) * 2, 2 + (j % 2) * 2
                        for (pbi, col0) in ((pg, j * 128), (pu, FF + j * 128)):
                            blk = col0 // 512
                            for kc in range(NCH):
                                op("pe", lambda e, pbi=pbi, col0=col0, kc=kc: e.matmul(
                                    pbank[pbi][:, :], lhsT=wgu[:, kc, col0:col0 + 128], rhs=ht[:, kc, :],
                                    start=(kc == 0), stop=(kc == NCH - 1)), reads=[b_wgu[blk], bht], writes=[b_pb[pbi]])
                        s_ = sg[j % 2]; bs = b_sg[j % 2]
                        op("act", lambda e, s_=s_, pg=pg: e.activation(out=s_[:], in_=pbank[pg][:, :], func=AF.Silu),
                           reads=[b_pb[pg]], writes=[bs])
                        op("dve", lambda e, s_=s_, pu=pu, j=j: e.tensor_tensor(out=act[:, j, :], in0=pbank[pu][:, :], in1=s_[:], op=ALU.mult),
                           reads=[b_pb[pu], bs], writes=[b_act[j]])
                    for dc in range(NCH):
                        ypi = 5 + dc % 2
                        xp_ = xp[xpi % 2]; bxp = b_xp[xpi % 2]; xpi += 1
                        dma("pool", xp_[:], xT[dc, :, t * 512:(t + 1) * 512], reads=[b_xTt[t][dc]], writes=[bxp])
                        if dc == 0:
                            for _ in nxt:
                                pass
                        for j in range(NFC):
                            op("pe", lambda e, ypi=ypi, j=j, dc=dc: e.matmul(
                                pbank[ypi][:, :], lhsT=wd[:, j, dc * 128:(dc + 1) * 128], rhs=act[:, j, :],
                                start=(j == 0), stop=(j == NFC - 1)), reads=[b_wd[j], b_act[j]], writes=[b_pb[ypi]])
                        op("dve", lambda e, ypi=ypi, dc=dc, xp_=xp_, slot=slot: e.scalar_tensor_tensor(
                            out=xp_[:], in0=pbank[ypi][:, :], scalar=mods[:, l, 40 + dc, slot:slot + 1], in1=xp_[:],
                            op0=ALU.mult, op1=ALU.add), reads=[b_pb[ypi], b_mods], writes=[bxp])
                        dma("pool", xT[dc, :, t * 512:(t + 1) * 512], xp_[:], reads=[bxp], writes=[b_xTt[t][dc]])
                k.end_phase(pbufs)

        with contextlib.ExitStack() as ph:
            pbufs = []
            def pbuf(n):
                b = k.buf(n); pbufs.append(b); return b
            xt = [sb(ph, f"xt{i}", [128, NCH, 512]) for i in range(2)]; bxt = [pbuf(f"xt{i}") for i in range(2)]
            hf = sb(ph, "hf", [128, NCH, 512]); bhf = pbuf("hf")
            RG = mk_ring(ph, pbuf, "z")
            yo = [sb(ph, f"yo{i}", [128, D]) for i in range(2)]; byo = [pbuf(f"yo{i}") for i in range(2)]
            for t in range(NTT):
                x_ = xt[t % 2]; bx = bxt[t % 2]
                dma("sp", x_[:], xT[:, :, t * 512:(t + 1) * 512].rearrange("c p t -> p c t"), reads=b_xTt[t], writes=[bx])
                slot = (t * 512) // SL
                norm_mod(x_, bx, hf, bhf, RG, L, 48, 0, slot, 0)
                for s in range(4):
                    tb = t * 4 + s
                    y_ = yo[tb % 2]; by = byo[tb % 2]
                    for half in range(2):
                        pbi = 1 + (tb * 2 + half) % 6
                        for c4 in range(4):
                            c = half * 4 + c4
                            op("pe", lambda e, c=c, c4=c4, s=s, pbi=pbi: e.transpose(
                                out=pbank[pbi][:, c4 * 128:(c4 + 1) * 128], in_=hf[:, c, s * 128:(s + 1) * 128], identity=ident_f),
                               reads=[bhf, b_cst], writes=[b_pb[pbi]])
                        if half == 0:
                            op("act", lambda e, y_=y_, pbi=pbi: e.copy(out=y_[:, 0:512], in_=pbank[pbi][:, :]), reads=[b_pb[pbi]], writes=[by])
                        else:
                            op("dve", lambda e, y_=y_, pbi=pbi: e.tensor_copy(out=y_[:, 512:1024], in_=pbank[pbi][:, :]), reads=[b_pb[pbi]], writes=[by])
                    dma("sp", y_out[tb * 128:(tb + 1) * 128, :], y_[:], reads=[by], writes=[])
            k.end_phase(pbufs)
        if os.environ.get("KDBG_MODS"):
            dbg = nc.dram_tensor("dbg", [128, (L + 1) * 144], F32, kind="ExternalOutput").ap()
            dma("sp", dbg, mods[:].rearrange("p l n s -> p (l n s)"), reads=[b_mods])
        k.finish()
    return nc


_CACHE = {}


def _consts():
    c = np.zeros((128, 8, 128), np.float32)
    c[:, 0, :] = np.eye(128)
    c[:, 1, :] = 1.0
    j = np.arange(64)[:, None]; i = np.arange(64)[None, :]
    c[:64, 2, :64] = (j <= i)
    c[:64, 3, :64] = (j >= i)
    c[:64, 4, :64] = (j > i)
    c[:64, 5, :64] = (j < i)
    c[:64, 6, :64] = np.where(j <= i, 0.0, -30000.0)
    c[:64, 7, :64] = np.where(j >= i, 0.0, -30000.0)
    return c


def _dft_mats(NT, nseq):
    S = NT // nseq
    n = np.arange(S)
    ang = 2.0 * np.pi * ((n[:, None] * n[None, :]) % S) / S
    sc = 1.0 / np.sqrt(S)
    Cb = np.cos(ang) * sc
    Sb = -np.sin(ang) * sc
    Mc = np.zeros((NT, NT), np.float32); Ms = np.zeros((NT, NT), np.float32)
    for q in range(nseq):
        Mc[q * S:(q + 1) * S, q * S:(q + 1) * S] = Cb
        Ms[q * S:(q + 1) * S, q * S:(q + 1) * S] = Sb
    NB = NT // 128
    def lay(M):
        return np.ascontiguousarray(M.reshape(NB, 128, NB, 128).transpose(2, 1, 0, 3)).astype(ml_dtypes.bfloat16)
    return lay(Mc), lay(Ms)


def _cdft():
    n = np.arange(128)
    ang = 2.0 * np.pi * ((n[:, None] * n[None, :]) % 128) / 128
    sc = 1.0 / np.sqrt(128.0)
    return np.concatenate([np.cos(ang) * sc, np.sin(ang) * sc], axis=1).astype(ml_dtypes.bfloat16)


def run_cores(core_x, core_c, nseqs, W, NT, layer_types):
    L = len(layer_types)
    key = (NT, tuple(layer_types))
    if key not in _CACHE:
        _CACHE[key] = build(NT, layer_types)
    nc = _CACHE[key]
    f32 = lambda a: np.ascontiguousarray(a, dtype=np.float32)
    colT = lambda v: f32(np.asarray(v).reshape(-1, 128).T)
    shared = {
        "ada_w": f32(W["ada_w"][:L]),
        "ada_bT": f32(np.stack([colT(W["ada_b"][i]) for i in range(L)])),
        "nmgT": f32(np.stack([colT(W["norm_mix_g"][i]) for i in range(L)])),
        "nfgT": f32(np.stack([colT(W["norm_ffn_g"][i]) for i in range(L)])),
        "fada_w": f32(W["final_ada_w"]),
        "fada_bT": colT(W["final_ada_b"]),
        "fngT": colT(W["final_norm_g"]),
        "fnet_w": f32(W["fnet_w"]),
        "fnet_bT": f32(np.stack([colT(W["fnet_b"][i]) for i in range(W["fnet_w"].shape[0])])),
        "gdn_w_in": f32(W["gdn_w_in"]),
        "conv_wT": f32(np.stack([np.asarray(W["gdn_conv_w"][i]).T.reshape(24, 128, 5).transpose(1, 0, 2)
                                 for i in range(W["gdn_w_in"].shape[0])])),
        "alog_row": f32(np.asarray(W["gdn_a_log"]).reshape(-1, 16)),
        "dtb_col": f32(np.concatenate([np.asarray(W["gdn_dt_bias"]).reshape(-1, 16), np.zeros((W["gdn_w_in"].shape[0], 16), np.float32)], axis=1)[:, :, None]),
        "gng_col": f32(np.asarray(W["gdn_norm_g"])[:, :, None]),
        "gdn_w_out": f32(W["gdn_w_out"]),
        "ffn_w_gu": f32(W["ffn_w_gu"][:L]),
        "ffn_w_down": f32(W["ffn_w_down"][:L]),
        "cdft": _cdft(),
        "cf32": _consts(),
    }
    dft = {n: _dft_mats(NT, n) for n in set(nseqs)}
    in_maps = []
    for x, c, n in zip(core_x, core_c, nseqs):
        m = dict(shared)
        m["x_in"] = f32(x)
        m["cT"] = f32(np.asarray(c).reshape(2, NCH, 128).transpose(2, 1, 0))
        m["dftc"], m["dfts"] = dft[n]
        m["flag"] = np.full((128, 1), 1.0 if n == 1 else 0.0, np.float32)
        in_maps.append(m)
    if os.environ.get("KDBG_TRACE"):
        res = run_bass_kernel_spmd(nc, in_maps, core_ids=list(range(len(in_maps))), trace=True)
        print("EXEC_TIME_NS", res.exec_time_ns)
    else:
        res = run_bass_kernel_spmd(nc, in_maps, core_ids=list(range(len(in_maps))))
    if os.environ.get("KDBG_MODS"):
        run_cores.dbg = [r["dbg"] for r in res.results]
    return [r["y_out"] for r in res.results]


def kernel(x_prompt, x_sample, c_prompt, c_sample, ada_w, ada_b, norm_mix_g, norm_ffn_g,
           fnet_w, fnet_b, gdn_w_in, gdn_conv_w, gdn_a_log, gdn_dt_bias, gdn_norm_g, gdn_w_out,
           ffn_w_gu, ffn_w_down, final_ada_w, final_ada_b, final_norm_g):
    W = dict(ada_w=ada_w, ada_b=ada_b, norm_mix_g=norm_mix_g, norm_ffn_g=norm_ffn_g, fnet_w=fnet_w, fnet_b=fnet_b,
             gdn_w_in=gdn_w_in, gdn_conv_w=gdn_conv_w, gdn_a_log=gdn_a_log, gdn_dt_bias=gdn_dt_bias,
             gdn_norm_g=gdn_norm_g, gdn_w_out=gdn_w_out, ffn_w_gu=ffn_w_gu, ffn_w_down=ffn_w_down,
             final_ada_w=final_ada_w, final_ada_b=final_ada_b, final_norm_g=final_norm_g)
    W = {kk: np.asarray(v) for kk, v in W.items()}
    x_prompt = np.asarray(x_prompt); x_sample = np.asarray(x_sample)
    c_prompt = np.asarray(c_prompt); c_sample = np.asarray(c_sample)
    core_x, core_c, nseqs = [], [], []
    for b in range(4):
        core_x.append(x_prompt[b]); core_c.append(np.stack([c_prompt[b], c_prompt[b]])); nseqs.append(1)
    for b in range(4):
        core_x.append(x_sample[2 * b:2 * b + 2].reshape(4096, D)); core_c.append(c_sample[2 * b:2 * b + 2]); nseqs.append(2)
    ys = run_cores(core_x, core_c, nseqs, W, 4096, ["f", "g", "f", "g"])
    y_prompt = np.stack(ys[:4]).astype(np.float32)
    y_sample = np.concatenate([y.reshape(2, 2048, D) for y in ys[4:]], axis=0).astype(np.float32)
    return (y_prompt, y_sample)
```

```python
import contextlib
import os
import numpy as np
import ml_dtypes
import concourse.bass as bass
import concourse.mybir as mybir
from concourse.bass_utils import run_bass_kernel_spmd

F32, BF16 = mybir.dt.float32, mybir.dt.bfloat16
F32R = mybir.dt.float32r
AF = mybir.ActivationFunctionType
ALU = mybir.AluOpType
AX = mybir.AxisListType

D = 1024
NCH = 8
FF = 2816
NFC = 22
EPS = 1e-6
GP = 4128
CH = 64


class Eng:
    def __init__(self, name, h, sem):
        self.name, self.h, self.sem = name, h, sem
        self.count = 0
        self.seen = {}


class DSem:
    def __init__(self, sem):
        self.sem = sem
        self.count = 0
        self.last = None


class Buf:
    __slots__ = ("w", "r", "name", "excl")

    def __init__(self, name, init=None):
        self.name = name
        self.excl = False
        self.w = {}
        self.r = dict(init) if init else {}


class K:
    def __init__(self, nc, es, ndma=24):
        self.nc = nc
        self.es = es
        mk = lambda n: es.enter_context(nc.semaphore(n))
        self.eng = {
            "pe": Eng("pe", nc.tensor, mk("s_pe")),
            "act": Eng("act", nc.scalar, mk("s_act")),
            "dve": Eng("dve", nc.vector, mk("s_dve")),
            "pool": Eng("pool", nc.gpsimd, mk("s_pool")),
            "sp": Eng("sp", nc.sync, mk("s_sp")),
        }
        self.dsems = {q: [DSem(mk(f"d_{q}{i}")) for i in range(ndma if q != "act" else 8)] for q in ("sp", "pool", "act")}
        self.di = {"sp": 0, "pool": 0, "act": 0}
        self.all_bufs = []
        self.phase_tok = {}

    def buf(self, name):
        b = Buf(name, self.phase_tok)
        self.all_bufs.append(b)
        return b

    def end_phase(self, bufs):
        tok = dict(self.phase_tok)
        for b in bufs:
            for s, v in list(b.w.items()) + list(b.r.items()):
                tok[s] = max(tok.get(s, 0), v)
        self.phase_tok = tok

    def _wait(self, E, deps):
        for s, v in deps.items():
            if s is E and E.name == "pe":
                continue
            if E.seen.get(s, 0) >= v:
                continue
            E.h.wait_ge(s.sem, v)
            E.seen[s] = v

    def _deps(self, reads, writes, par=False):
        deps = {}

        def add(s, v):
            if deps.get(s, 0) < v:
                deps[s] = v

        for b in reads:
            for s, v in b.w.items():
                add(s, v)
        for b in writes:
            for s, v in b.w.items():
                if par and isinstance(s, DSem):
                    continue
                add(s, v)
            for s, v in b.r.items():
                add(s, v)
        return deps

    def _mark(self, tok, reads, writes, par=False):
        s, v = tok
        for b in reads:
            if b.r.get(s, 0) < v:
                b.r[s] = v
        for b in writes:
            if par:
                b.w[s] = v
            else:
                b.w = {s: v}
            b.r = {}

    def op(self, e, fn, reads=(), writes=()):
        E = self.eng[e]
        ex = [b for b in reads if b.excl]
        if ex:
            writes = list(writes) + [b for b in ex if b not in writes]
            reads = [b for b in reads if not b.excl]
        self._wait(E, self._deps(reads, writes))
        ins = fn(E.h)
        E.count += 1
        ins.then_inc(E.sem, 1)
        tok = (E, E.count)
        self._mark(tok, reads, writes)
        return tok

    def dma(self, q, out, in_, reads=(), writes=(), par=False):
        E = self.eng[q]
        ds = self.dsems[q][self.di[q] % len(self.dsems[q])]
        self.di[q] += 1
        deps = self._deps(reads, writes, par)
        if ds.count:
            deps[ds] = max(deps.get(ds, 0), ds.count)
        self._wait(E, deps)
        ins = E.h.dma_start(out=out, in_=in_)
        ds.count += 16
        ins.then_inc(ds.sem, 16)
        tok = (ds, ds.count)
        self._mark(tok, reads, writes, par)
        return tok

    def finish(self):
        E = self.eng["sp"]
        deps = {}
        for q in self.dsems:
            for ds in self.dsems[q]:
                if ds.count:
                    deps[ds] = ds.count
        for n, e in self.eng.items():
            if e.count and n != "sp":
                deps[e] = e.count
        self._wait(E, deps)


def build(NT, layer_types, n_final=True):
    nc = bass.Bass("TRN2", target_bir_lowering=False)
    L = len(layer_types)
    NTT = NT // 512
    NB = NT // 128
    SL = NT // 2
    nF = max(1, sum(1 for t in layer_types if t == "f"))
    nG = max(1, sum(1 for t in layer_types if t == "g"))

    def din(name, shape, dt=F32):
        return nc.dram_tensor(name, list(shape), dt, kind="ExternalInput").ap()

    def dscr(name, shape, dt=F32):
        return nc.dram_tensor(name, list(shape), dt, kind="Internal").ap()

    x_in = din("x_in", [NT, D])
    cT = din("cT", [128, NCH, 2])
    ada_w = din("ada_w", [L, D, 6 * D])
    ada_bT = din("ada_bT", [L, 128, 48])
    nmgT = din("nmgT", [L, 128, NCH])
    nfgT = din("nfgT", [L, 128, NCH])
    fada_w = din("fada_w", [D, 2 * D])
    fada_bT = din("fada_bT", [128, 16])
    fngT = din("fngT", [128, NCH])
    fnet_w = din("fnet_w", [nF, D, D])
    fnet_bT = din("fnet_bT", [nF, 128, NCH])
    gdn_w_in = din("gdn_w_in", [nG, D, GP])
    conv_wT = din("conv_wT", [nG, 128, 24, 5])
    alog_row = din("alog_row", [nG, 16])
    dtb_col = din("dtb_col", [nG, 32, 1])
    gng_col = din("gng_col", [nG, 128, 1])
    gdn_w_out = din("gdn_w_out", [nG, D, D])
    ffn_w_gu = din("ffn_w_gu", [L, D, 2 * FF])
    ffn_w_down = din("ffn_w_down", [L, FF, D])
    dftc = din("dftc", [NB, 128, NB, 128], BF16)
    dfts = din("dfts", [NB, 128, NB, 128], BF16)
    cdft = din("cdft", [128, 256], BF16)
    cf32 = din("cf32", [128, 8, 128])
    flag = din("flag", [128, 1])
    y_out = nc.dram_tensor("y_out", [NT, D], F32, kind="ExternalOutput").ap()

    xT = dscr("xT", [NCH, 128, NT])
    P_s = dscr("P_s", [24, 128, NT])
    qT_s = dscr("qT_s", [8, 128, NT], BF16)
    kT_s = dscr("kT_s", [8, 128, NT], BF16)
    zT_s = dscr("zT_s", [8, 128, NT], BF16)
    ktm_s = dscr("ktm_s", [NT, D], BF16)
    vtm_s = dscr("vtm_s", [NT, D], BF16)
    gb_s = dscr("gb_s", [NT, 32])
    o_s = dscr("o_s", [NT, D])

    es = contextlib.ExitStack()
    with es:
        k = K(nc, es)
        op, dma = k.op, k.dma

        uniq = [0]

        def sb(stack, name, shape, dt=F32):
            uniq[0] += 1
            return stack.enter_context(nc.sbuf_tensor(f"{name}_{uniq[0]}", list(shape), dt))

        cst = sb(es, "cst", [128, 8, 128]); b_cst = k.buf("cst")
        cstb = sb(es, "cstb", [128, 8, 128], BF16); b_cstb = k.buf("cstb")
        mods = sb(es, "mods", [128, L + 1, 72, 2]); b_mods = k.buf("mods")
        csT = sb(es, "csT", [128, NCH, 2], BF16); b_csT = k.buf("csT")
        flg = sb(es, "flg", [128, 1]); b_flg = k.buf("flg")
        epsc = sb(es, "epsc", [128, 1]); b_eps = k.buf("epsc")
        pbank = [es.enter_context(nc.psum_tensor(f"pb{i}", [128, 512], F32)) for i in range(8)]
        b_pb = [k.buf(f"pb{i}") for i in range(8)]
        for b_ in b_pb:
            b_.excl = True

        C_ID, C_ONES, C_MCF, C_MCB, C_M2F, C_M2B, C_NMF, C_NMB = range(8)
        dma("sp", cst[:], cf32, writes=[b_cst])
        dma("sp", flg[:], flag, writes=[b_flg])
        op("dve", lambda e: e.tensor_copy(out=cstb[:], in_=cst[:]), reads=[b_cst], writes=[b_cstb])
        op("pool", lambda e: e.memset(epsc[:], EPS), writes=[b_eps])
        ident_b = cstb[:, C_ID, :]
        ident_f = cst[:, C_ID, :]
        ones_b = cstb[:, C_ONES, :]
        ones_f = cst[:, C_ONES, :]

        b_Ps = [k.buf(f"Ps{t}") for t in range(NTT)]; b_zs = [k.buf(f"zs{t}") for t in range(NTT)]
        b_gbs = [k.buf(f"gbs{t}") for t in range(NTT)]; b_qs = [k.buf(f"qs{t}") for t in range(NTT)]
        b_ks = [k.buf(f"ks{t}") for t in range(NTT)]; b_ktm = [k.buf(f"ktm{t}") for t in range(NTT)]
        b_vtm = [k.buf(f"vtm{t}") for t in range(NTT)]; b_os = [k.buf(f"os{t}") for t in range(NTT)]
        b_xTt = [[k.buf(f"xT{t}_{c}") for c in range(NCH)] for t in range(NTT)]
        with contextlib.ExitStack() as ph:
            pbufs = []
            def pbuf(n):
                b = k.buf(n); pbufs.append(b); return b
            ct = sb(ph, "ct", [128, NCH, 2]); b_ct = pbuf("ct")
            dma("sp", ct[:], cT, writes=[b_ct])
            op("act", lambda e: e.activation(out=csT[:], in_=ct[:], func=AF.Silu), reads=[b_ct], writes=[b_csT])
            wb = [sb(ph, f"adaw{i}", [128, NCH, 768], BF16) for i in range(2)]
            b_wb = [pbuf(f"adaw{i}") for i in range(2)]
            abt = sb(ph, "abt", [128, L + 1, 48]); b_abt = pbuf("abt")
            gT = sb(ph, "gT", [128, L + 1, 2, NCH]); b_gT = pbuf("gT")
            fbT = sb(ph, "fbT", [128, nF, NCH]); b_fbT = pbuf("fbT")
            op("pool", lambda e: e.memset(abt[:], 0.0), writes=[b_abt])
            for l in range(L):
                dma("sp", abt[:, l, :], ada_bT[l], writes=[b_abt])
                dma("sp", gT[:, l, 0, :], nmgT[l], writes=[b_gT])
                dma("sp", gT[:, l, 1, :], nfgT[l], writes=[b_gT])
            dma("sp", abt[:, L, 0:16], fada_bT, writes=[b_abt])
            dma("sp", gT[:, L, 0, :], fngT, writes=[b_gT])
            dma("sp", gT[:, L, 1, :], fngT, writes=[b_gT])
            for i in range(nF):
                dma("sp", fbT[:, i, :], fnet_bT[i], writes=[b_fbT])
            wst = [sb(ph, f"adast{i}", [128, NCH, 768]) for i in range(2)]
            b_wst = [pbuf(f"adast{i}") for i in range(2)]
            xin = [sb(ph, f"xin{i}", [128, D]) for i in range(2)]
            b_xin = [pbuf(f"xin{i}") for i in range(2)]
            xst = [sb(ph, f"xst{i}", [128, NCH, 512]) for i in range(2)]
            b_xst = [pbuf(f"xst{i}") for i in range(2)]

            def emit_xblock(tb):
                xi = xin[tb % 2]; bxi = b_xin[tb % 2]
                dma("pool", xi[:], x_in[tb * 128:(tb + 1) * 128, :], writes=[bxi])
                st = xst[(tb // 4) % 2]; bst = b_xst[(tb // 4) % 2]
                for half in range(2):
                    pb_i = 2 + (tb * 2 + half) % 4
                    for c4 in range(4):
                        c = half * 4 + c4
                        op("pe", lambda e, xi=xi, c=c, c4=c4, pb_i=pb_i: e.transpose(
                            out=pbank[pb_i][:, c4 * 128:(c4 + 1) * 128], in_=xi[:, c * 128:(c + 1) * 128], identity=ident_f),
                           reads=[bxi, b_cst], writes=[b_pb[pb_i]])
                    if half == 0:
                        op("act", lambda e, st=st, half=half, pb_i=pb_i, tb=tb: e.copy(
                            out=st[:, half * 4:half * 4 + 4, (tb % 4) * 128:(tb % 4 + 1) * 128],
                            in_=pbank[pb_i][:].rearrange("p (c t) -> p c t", c=4)),
                           reads=[b_pb[pb_i]], writes=[bst])
                    else:
                        op("dve", lambda e, st=st, half=half, pb_i=pb_i, tb=tb: e.tensor_copy(
                            out=st[:, half * 4:half * 4 + 4, (tb % 4) * 128:(tb % 4 + 1) * 128],
                            in_=pbank[pb_i][:].rearrange("p (c t) -> p c t", c=4)),
                           reads=[b_pb[pb_i]], writes=[bst])
                if tb % 4 == 3:
                    t0 = (tb // 4) * 512
                    dma("pool", xT[:, :, t0:t0 + 512].rearrange("c p t -> p c t"), st[:], reads=[bst], writes=b_xTt[tb // 4])

            pi = 0
            xb_next = 0
            for l in range(L + 1):
                src = ada_w[l] if l < L else fada_w
                ncols = 6 * D if l < L else 2 * D
                pm = pbank[l % 2]
                bpm = b_pb[l % 2]
                nchunks = ncols // 128
                for pc in range((ncols + 767) // 768):
                    c0 = pc * 768
                    cw = min(768, ncols - c0)
                    w_ = wb[pi % 2]; bw = b_wb[pi % 2]
                    ws_ = wst[pi % 2]; bws = b_wst[pi % 2]; pi += 1
                    for kc in range(NCH):
                        dma("sp", ws_[:, kc, 0:cw], src[kc * 128:(kc + 1) * 128, c0:c0 + cw], writes=[bws], par=(kc > 0))
                    op("act", lambda e, w_=w_, ws_=ws_, cw=cw: e.copy(out=w_[:, 0:4, 0:cw], in_=ws_[:, 0:4, 0:cw]), reads=[bws], writes=[bw])
                    op("dve", lambda e, w_=w_, ws_=ws_, cw=cw: e.tensor_copy(out=w_[:, 4:8, 0:cw], in_=ws_[:, 4:8, 0:cw]), reads=[bws], writes=[bw])
                    for ncx in range(cw // 128):
                        n_abs = c0 // 128 + ncx
                        for kc in range(NCH):
                            op("pe", lambda e, w_=w_, kc=kc, ncx=ncx, n_abs=n_abs, pm=pm: e.matmul(
                                pm[:, n_abs * 2:n_abs * 2 + 2], lhsT=w_[:, kc, ncx * 128:(ncx + 1) * 128],
                                rhs=csT[:, kc, :], start=(kc == 0), stop=(kc == NCH - 1)),
                               reads=[bw, b_csT], writes=[bpm])
                    if xb_next < NB:
                        emit_xblock(xb_next); xb_next += 1
                op("dve", lambda e, l=l, pm=pm, nchunks=nchunks: e.tensor_tensor(
                    out=mods[:, l, 0:nchunks, :], in0=pm[:, 0:nchunks * 2].rearrange("p (n s) -> p n s", s=2),
                    in1=abt[:, l, 0:nchunks].unsqueeze(2).to_broadcast([128, nchunks, 2]), op=ALU.add),
                   reads=[bpm, b_abt], writes=[b_mods])
                if l < L:
                    for (dst, scidx, gi_) in ((48, 8, 0), (56, 32, 1)):
                        op("dve", lambda e, l=l, dst=dst, scidx=scidx, gi_=gi_: e.scalar_tensor_tensor(
                            out=mods[:, l, dst:dst + 8, :], in0=mods[:, l, scidx:scidx + 8, :], scalar=1.0,
                            in1=gT[:, l, gi_, :].unsqueeze(2).to_broadcast([128, NCH, 2]), op0=ALU.add, op1=ALU.mult),
                           reads=[b_gT], writes=[b_mods])
                    if layer_types[l] == "f":
                        fi = sum(1 for t in layer_types[:l] if t == "f")
                        op("dve", lambda e, l=l, fi=fi: e.tensor_tensor(
                            out=mods[:, l, 64:72, :], in0=mods[:, l, 16:24, :],
                            in1=fbT[:, fi, :].unsqueeze(2).to_broadcast([128, NCH, 2]), op=ALU.mult),
                           reads=[b_fbT], writes=[b_mods])
                else:
                    op("dve", lambda e, l=l: e.scalar_tensor_tensor(
                        out=mods[:, l, 48:56, :], in0=mods[:, l, 8:16, :], scalar=1.0,
                        in1=gT[:, l, 0, :].unsqueeze(2).to_broadcast([128, NCH, 2]), op0=ALU.add, op1=ALU.mult),
                       reads=[b_gT], writes=[b_mods])
            while xb_next < NB:
                emit_xblock(xb_next); xb_next += 1
            k.end_phase(pbufs)

        def mk_ring(stack, pbuf, pre, with_tmp=True, nsq=2):
            R = {}
            R["sq"] = [(sb(stack, f"{pre}sq{i}", [128, 512], BF16), pbuf(f"{pre}sq{i}")) for i in range(nsq)]
            if with_tmp:
                R["tmp"] = [(sb(stack, f"{pre}tmp{i}", [128, 512]), pbuf(f"{pre}tmp{i}")) for i in range(2)]
            R["rs"] = (sb(stack, f"{pre}rs", [128, 512]), pbuf(f"{pre}rs"))
            return R

        def norm_mod(xt, bxt, ht, bht, R, l, aidx, shidx, slot, ps_i):
            for _ in norm_mod_gen(xt, bxt, ht, bht, R, l, aidx, shidx, slot, ps_i):
                pass

        def norm_mod_gen(xt, bxt, ht, bht, R, l, aidx, shidx, slot, ps_i, inplace=False):
            nsq = len(R["sq"])
            if nsq >= NCH:
                for c in range(NCH):
                    sq, bsq = R["sq"][c]
                    op("act", lambda e, sq=sq, c=c: e.activation(out=sq[:], in_=xt[:, c, :], func=AF.Square), reads=[bxt], writes=[bsq])
                    if c % 2 == 1:
                        yield
                yield
                for c in range(NCH):
                    sq, bsq = R["sq"][c]
                    op("pe", lambda e, sq=sq, c=c: e.matmul(pbank[ps_i][:, :], lhsT=ones_b, rhs=sq[:],
                                                            start=(c == 0), stop=(c == NCH - 1)),
                       reads=[bsq, b_cstb], writes=[b_pb[ps_i]])
                yield
            else:
                for c in range(NCH):
                    sq, bsq = R["sq"][c % nsq]
                    op("act", lambda e, sq=sq, c=c: e.activation(out=sq[:], in_=xt[:, c, :], func=AF.Square), reads=[bxt], writes=[bsq])
                    op("pe", lambda e, sq=sq, c=c: e.matmul(pbank[ps_i][:, :], lhsT=ones_b, rhs=sq[:],
                                                            start=(c == 0), stop=(c == NCH - 1)),
                       reads=[bsq, b_cstb], writes=[b_pb[ps_i]])
                    yield
            rs, brs = R["rs"]
            op("act", lambda e: e.activation(out=rs[:], in_=pbank[ps_i][:, :], func=AF.Sqrt, bias=epsc[:, 0:1], scale=1.0 / D),
               reads=[b_pb[ps_i], b_eps], writes=[brs])
            op("dve", lambda e: e.reciprocal(out=rs[:], in_=rs[:]), reads=[], writes=[brs])
            yield
            for c in range(NCH):
                if inplace:
                    tmp, btmp = xt[:, c, :], bxt
                    op("dve", lambda e, tmp=tmp, c=c: e.tensor_tensor(out=tmp, in0=xt[:, c, :], in1=rs[:], op=ALU.mult),
                       reads=[brs], writes=[bxt])
                else:
                    tmp, btmp = R["tmp"][c % 2]
                    tmp = tmp[:]
                    op("dve", lambda e, tmp=tmp, c=c: e.tensor_tensor(out=tmp, in0=xt[:, c, :], in1=rs[:], op=ALU.mult),
                       reads=[bxt, brs], writes=[btmp])
                op("act", lambda e, tmp=tmp, c=c: e.activation(out=ht[:, c, :], in_=tmp, func=AF.Identity,
                                                               scale=mods[:, l, aidx + c, slot:slot + 1],
                                                               bias=mods[:, l, shidx + c, slot:slot + 1]),
                   reads=[btmp, b_mods], writes=[bht])
                yield

        def gdn_layer(l, gi):
            NG = NT // 512
            g1idx = 16
            with contextlib.ExitStack() as ph:
                pbufs = []
                def pbuf(n):
                    b = k.buf(n); pbufs.append(b); return b
                win = sb(ph, "win", [128, NCH, GP], BF16); b_win = pbuf("win")
                for kc in range(NCH):
                    dma("pool", win[:, kc, :], gdn_w_in[gi, kc * 128:(kc + 1) * 128, :], writes=[b_win], par=(kc > 0))
                xt = sb(ph, "gxt", [128, NCH, 512]); bx = pbuf("gxt")
                hts = [sb(ph, f"ght{i}", [128, NCH, 512], BF16) for i in range(2)]; bhts = [pbuf(f"ght{i}") for i in range(2)]
                RG = mk_ring(ph, pbuf, "g", nsq=8)

                def ga_norm_gen(t):
                    dma("sp", xt[:], xT[:, :, t * 512:(t + 1) * 512].rearrange("c p t -> p c t"), reads=b_xTt[t], writes=[bx])
                    yield
                    yield from norm_mod_gen(xt, bx, hts[t % 2], bhts[t % 2], RG, l, 48, 0, (t * 512) // SL, 0)
                pst = [sb(ph, f"pst{i}", [128, 4, 512], F32R) for i in range(2)]; b_pst = [pbuf(f"pst{i}") for i in range(2)]
                zst = sb(ph, "zst", [128, 8, 512], BF16); b_zst = pbuf("zst")
                dtb = sb(ph, "dtb", [32, 1]); b_dtb = pbuf("dtb")
                nal = sb(ph, "nal", [128, 16]); b_nal = pbuf("nal")
                dma("sp", dtb[:], dtb_col[gi], writes=[b_dtb])
                dma("sp", nal[:], alog_row[gi:gi + 1, :].to_broadcast([128, 16]), writes=[b_nal])
                op("act", lambda e: e.activation(out=nal[:], in_=nal[:], func=AF.Exp), writes=[b_nal])
                op("dve", lambda e: e.tensor_scalar(out=nal[:], in0=nal[:], scalar1=-1.0, scalar2=None, op0=ALU.mult), writes=[b_nal])
                gE = sb(ph, "gE", [32, 512]); b_gE = pbuf("gE")
                gA = sb(ph, "gA", [32, 512]); b_gA = pbuf("gA")
                gX = sb(ph, "gX", [32, 512]); b_gX = pbuf("gX")
                gS = sb(ph, "gS", [32, 512]); b_gS = pbuf("gS")
                gbt = sb(ph, "gbt", [128, 4, 32]); b_gbt = pbuf("gbt")
                for _ in ga_norm_gen(0):
                    pass
                for t in range(NTT):
                    t0 = t * 512
                    ht = hts[t % 2]; bht = bhts[t % 2]
                    nxt = ga_norm_gen(t + 1) if t + 1 < NTT else iter(())
                    for c in range(33):
                        if c >= 4:
                            next(nxt, None)
                        pbi = 1 + c % 4
                        rows = 128 if c < 32 else 32
                        for kc in range(NCH):
                            op("pe", lambda e, c=c, kc=kc, pbi=pbi, rows=rows: e.matmul(
                                pbank[pbi][0:rows, :], lhsT=win[:, kc, c * 128:c * 128 + rows], rhs=ht[:, kc, :],
                                start=(kc == 0), stop=(kc == NCH - 1)), reads=[b_win, bht], writes=[b_pb[pbi]])
                        if c < 24:
                            ps_ = pst[(c // 4) % 2]; bps = b_pst[(c // 4) % 2]
                            if c % 2 == 0:
                                op("act", lambda e, ps_=ps_, c=c, pbi=pbi: e.copy(out=ps_[:, c % 4, :], in_=pbank[pbi][:, :]),
                                   reads=[b_pb[pbi]], writes=[bps])
                            else:
                                op("dve", lambda e, ps_=ps_, c=c, pbi=pbi: e.tensor_copy(out=ps_[:, c % 4, :], in_=pbank[pbi][:, :]),
                                   reads=[b_pb[pbi]], writes=[bps])
                            if c % 4 == 3:
                                c0 = c - 3
                                dma("pool", P_s[c0:c0 + 4, :, t0:t0 + 512].rearrange("c p t -> p c t"), ps_[:].bitcast(F32), reads=[bps], writes=[b_Ps[t]], par=True)
                        elif c < 32:
                            op("act", lambda e, c=c, pbi=pbi: e.activation(out=zst[:, c - 24, :], in_=pbank[pbi][:, :], func=AF.Silu),
                               reads=[b_pb[pbi]], writes=[b_zst])
                            if c == 31:
                                dma("pool", zT_s[:, :, t0:t0 + 512].rearrange("c p t -> p c t"), zst[:], reads=[b_zst], writes=[b_zs[t]])
                        else:
                            pg = pbank[pbi][0:32, :]
                            op("act", lambda e, pg=pg: e.activation(out=gE[:], in_=pg, func=AF.Identity, bias=dtb[:, 0:1], scale=1.0),
                               reads=[b_pb[pbi], b_dtb], writes=[b_gE])
                            op("act", lambda e, pg=pg: e.activation(out=gS[:], in_=pg, func=AF.Sigmoid), reads=[b_pb[pbi]], writes=[b_gS])
                            op("act", lambda e: e.activation(out=gA[:], in_=gE[:], func=AF.Abs), reads=[b_gE], writes=[b_gA])
                            op("act", lambda e: e.activation(out=gA[:], in_=gA[:], func=AF.Exp, scale=-1.0), writes=[b_gA])
                            op("act", lambda e: e.activation(out=gA[:], in_=gA[:], func=AF.Ln, bias=1.0, scale=1.0), writes=[b_gA])
                            op("dve", lambda e: e.tensor_scalar(out=gX[:], in0=gE[:], scalar1=0.0, scalar2=None, op0=ALU.max), reads=[b_gE], writes=[b_gX])
                            op("dve", lambda e: e.tensor_tensor(out=gX[:], in0=gX[:], in1=gA[:], op=ALU.add), reads=[b_gA], writes=[b_gX])
                            gp = 1 + (c + 1) % 4
                            for s_ in range(4):
                                op("pe", lambda e, s_=s_, gp=gp: e.transpose(out=pbank[gp][:, s_ * 64:s_ * 64 + 32], in_=gX[:, s_ * 128:(s_ + 1) * 128],
                                                                          identity=ident_f[0:32, 0:32]), reads=[b_gX, b_cst], writes=[b_pb[gp]])
                                op("pe", lambda e, s_=s_, gp=gp: e.transpose(out=pbank[gp][:, s_ * 64 + 32:s_ * 64 + 64], in_=gS[:, s_ * 128:(s_ + 1) * 128],
                                                                          identity=ident_f[0:32, 0:32]), reads=[b_gS, b_cst], writes=[b_pb[gp]])
                            pv = pbank[gp][:, 0:256].rearrange("p (s x) -> p s x", s=4)
                            op("dve", lambda e, pv=pv: e.tensor_tensor(out=gbt[:, :, 0:16], in0=pv[:, :, 0:16],
                                                                       in1=nal[:].unsqueeze(1).to_broadcast([128, 4, 16]), op=ALU.mult),
                               reads=[b_pb[gp], b_nal], writes=[b_gbt])
                            op("act", lambda e, pv=pv: e.copy(out=gbt[:, :, 16:32], in_=pv[:, :, 48:64]), reads=[b_pb[gp]], writes=[b_gbt])
                            dma("pool", gb_s[t0:t0 + 512, :].rearrange("(s p) x -> p s x", p=128), gbt[:], reads=[b_gbt], writes=[b_gbs[t]])
                    for _ in nxt:
                        pass
                k.end_phase(pbufs)
            with contextlib.ExitStack() as ph:
                pbufs = []
                def pbuf(n):
                    b = k.buf(n); pbufs.append(b); return b
                cw = sb(ph, "cw", [128, 24, 5]); b_cw = pbuf("cw")
                dma("sp", cw[:], conv_wT[gi], writes=[b_cw])
                dg = sb(ph, "dg", [128, 24, 5, 128], F32R); b_dg = pbuf("dg")
                for c in range(24):
                    op("dve" if c % 2 == 0 else "pool", lambda e, c=c: e.tensor_tensor(
                        out=dg[:, c, :, :], in0=ident_f.unsqueeze(1).to_broadcast([128, 5, 128]),
                        in1=cw[:, c, :].unsqueeze(2).to_broadcast([128, 5, 128]), op=ALU.mult), reads=[b_cw, b_cst], writes=[b_dg])
                Ph = sb(ph, "Ph", [128, 24, 516], F32R); b_Php = [pbuf(f"Ph{i}") for i in range(3)]
                sa = sb(ph, "sa", [128, 16, 512]); b_sa = [pbuf(f"sa{i}") for i in range(16)]
                sq2 = [sb(ph, f"sqq{i}", [128, 512], BF16) for i in range(4)]; b_sq2 = [pbuf(f"sqq{i}") for i in range(4)]
                rq = [sb(ph, f"rq{i}", [128, 512]) for i in range(4)]; b_rq = [pbuf(f"rq{i}") for i in range(4)]
                qkv = sb(ph, "qkvT", [128, 24, 512], BF16); b_qkv = [pbuf(f"qkv{i}") for i in range(3)]
                tm = [sb(ph, f"tm{i}", [128, D], BF16) for i in range(2)]; b_tm = [pbuf(f"tm{i}") for i in range(2)]
                tmi = 0

                def load_ph(t, part):
                    t0 = t * 512
                    cs3 = slice(part * 8, part * 8 + 8)
                    bP = b_Php[part]
                    dma("sp", Ph[:, cs3, 2:514].bitcast(F32), P_s[cs3, :, t0:t0 + 512].rearrange("c p t -> p c t"), reads=[b_Ps[t]], writes=[bP])
                    with nc.allow_non_contiguous_dma(reason="conv halo"):
                        if t > 0:
                            dma("sp", Ph[:, cs3, 0:2].bitcast(F32), P_s[cs3, :, t0 - 2:t0].rearrange("c p t -> p c t"), reads=[b_Ps[t - 1]], writes=[bP], par=True)
                        if t < NTT - 1:
                            dma("sp", Ph[:, cs3, 514:516].bitcast(F32), P_s[cs3, :, t0 + 512:t0 + 514].rearrange("c p t -> p c t"), reads=[b_Ps[t + 1]], writes=[bP], par=True)
                    if t == 0:
                        op("pool", lambda e: e.memset(Ph[:, cs3, 0:2].bitcast(F32), 0.0), writes=[bP])
                    if t == NTT - 1:
                        op("pool", lambda e: e.memset(Ph[:, cs3, 514:516].bitcast(F32), 0.0), writes=[bP])
                    if t0 == SL:
                        op("pool", lambda e: e.tensor_scalar(out=Ph[:, cs3, 0:2], in0=Ph[:, cs3, 0:2], scalar1=flg[:, 0:1], scalar2=None, op0=ALU.mult),
                           reads=[b_flg], writes=[bP])
                    if t0 + 512 == SL:
                        op("pool", lambda e: e.tensor_scalar(out=Ph[:, cs3, 514:516], in0=Ph[:, cs3, 514:516], scalar1=flg[:, 0:1], scalar2=None, op0=ALU.mult),
                           reads=[b_flg], writes=[bP])

                for part in range(3):
                    load_ph(0, part)
                for t in range(NTT):
                    t0 = t * 512
                    for c in range(24):
                        pc = 1 + c % 4
                        for j in range(5):
                            op("pe", lambda e, c=c, j=j, pc=pc: e.matmul(pbank[pc][:, :], lhsT=dg[:, c, j, :], rhs=Ph[:, c, j:j + 512],
                                                                        start=(j == 0), stop=(j == 4)), reads=[b_dg, b_Php[c // 8]], writes=[b_pb[pc]])
                        if c < 16:
                            op("act", lambda e, c=c, pc=pc: e.activation(out=sa[:, c, :], in_=pbank[pc][:, :], func=AF.Silu), reads=[b_pb[pc]], writes=[b_sa[c]])
                        else:
                            op("act", lambda e, c=c, pc=pc: e.activation(out=qkv[:, c, :], in_=pbank[pc][:, :], func=AF.Silu), reads=[b_pb[pc]], writes=[b_qkv[2]])
                        if c % 8 == 7 and t + 1 < NTT:
                            load_ph(t + 1, c // 8)
                    PN = [5, 6, 7, 0]
                    def st_sq(c):
                        s2 = sq2[c % 4]; bs2 = b_sq2[c % 4]
                        op("dve", lambda e: e.tensor_tensor(out=s2[:], in0=sa[:, c, :], in1=sa[:, c, :], op=ALU.mult), reads=[b_sa[c]], writes=[bs2])
                    def st_mm(c):
                        s2 = sq2[c % 4]; bs2 = b_sq2[c % 4]; pn = PN[c % 4]
                        op("pe", lambda e: e.matmul(pbank[pn][:, :], lhsT=ones_b, rhs=s2[:], start=True, stop=True), reads=[bs2, b_cstb], writes=[b_pb[pn]])
                    def st_ln(c):
                        r_ = rq[c % 4]; br = b_rq[c % 4]; pn = PN[c % 4]
                        op("act", lambda e: e.activation(out=r_[:], in_=pbank[pn][:, :], func=AF.Ln, bias=epsc[:, 0:1], scale=1.0),
                           reads=[b_pb[pn], b_eps], writes=[br])
                    def st_ex(c):
                        r_ = rq[c % 4]; br = b_rq[c % 4]
                        op("act", lambda e: e.activation(out=r_[:], in_=r_[:], func=AF.Exp, scale=-0.5), writes=[br])
                    def st_out(c):
                        r_ = rq[c % 4]; br = b_rq[c % 4]
                        scl = (128.0 ** -0.5) if c < 8 else 1.0
                        op("dve", lambda e: e.scalar_tensor_tensor(out=qkv[:, c, :], in0=sa[:, c, :], scalar=scl, in1=r_[:], op0=ALU.mult, op1=ALU.mult),
                           reads=[b_sa[c], br], writes=[b_qkv[c // 8]])
                    stages = [st_sq, st_mm, st_ln, st_ex, st_out]
                    for i in range(16 + len(stages) - 1):
                        for si in range(len(stages) - 1, -1, -1):
                            c = i - si
                            if 0 <= c < 16:
                                stages[si](c)
                    dma("pool", qT_s[:, :, t0:t0 + 512].rearrange("c p t -> p c t"), qkv[:, 0:8, :], reads=[b_qkv[0]], writes=[b_qs[t]])
                    dma("pool", kT_s[:, :, t0:t0 + 512].rearrange("c p t -> p c t"), qkv[:, 8:16, :], reads=[b_qkv[1]], writes=[b_ks[t]])
                    for (base, dst, bdst, bsrc) in ((8, ktm_s, b_ktm, b_qkv[1]), (16, vtm_s, b_vtm, b_qkv[2])):
                        for s_ in range(4):
                            tpi = 1 + tmi % 4
                            tp = pbank[tpi][:].bitcast(BF16)
                            for h in range(8):
                                op("pe", lambda e, h=h, s_=s_, base=base, tp=tp: e.transpose(
                                    out=tp[:, h * 128:(h + 1) * 128], in_=qkv[:, base + h, s_ * 128:(s_ + 1) * 128], identity=ident_b),
                                   reads=[bsrc, b_cstb], writes=[b_pb[tpi]])
                            tm_ = tm[tmi % 2]; btm = b_tm[tmi % 2]
                            if tmi % 2 == 0:
                                op("act", lambda e, tm_=tm_, tp=tp: e.copy(out=tm_[:], in_=tp), reads=[b_pb[tpi]], writes=[btm])
                            else:
                                op("dve", lambda e, tm_=tm_, tp=tp: e.tensor_copy(out=tm_[:], in_=tp), reads=[b_pb[tpi]], writes=[btm])
                            tmi += 1
                            dma("pool", dst[t0 + s_ * 128:t0 + (s_ + 1) * 128, :], tm_[:], reads=[btm], writes=[bdst[t]], par=True)
                k.end_phase(pbufs)
            with contextlib.ExitStack() as ph:
                pbufs = []
                def pbuf(n):
                    b = k.buf(n); pbufs.append(b); return b
                def T_(name, shape, dt=F32):
                    return sb(ph, name, shape, dt), pbuf(name)
                GC = 4
                NGR = NT // (GC * 64)
                NCHK = NT // 64
                R = 4
                NPAR = int(os.environ.get("KDBG_NPAR", "4"))
                wout, b_wout = T_("wout", [128, 8, D], BF16)
                for kc in range(NCH):
                    dma("pool", wout[:, kc, :], gdn_w_out[gi, kc * 128:(kc + 1) * 128, :], writes=[b_wout], par=(kc > 0))
                gng, b_gng = T_("gng", [128, 1])
                dma("sp", gng[:], gng_col[gi], writes=[b_gng])
                GW = GC * 64
                grp = []
                for i in range(2):
                    grp.append(dict(
                        q=T_(f"qTg{i}", [128, 8, GW], BF16), k=T_(f"kTg{i}", [128, 8, GW], BF16),
                        ktm=T_(f"ktmg{i}", [64, GC, D], BF16), vtm=T_(f"vtmg{i}", [64, GC, D], BF16),
                        gb=T_(f"gbg{i}", [64, GC, 32]), z=T_(f"zTg{i}", [128, 8, GW], BF16)))
                ring = []
                for i in range(R):
                    ring.append(dict(Y=T_(f"rY{i}", [64, 8, 64], BF16), QKd=T_(f"rQKd{i}", [64, 8, 64], BF16),
                                     qg=T_(f"rqg{i}", [128, 8, 64], BF16), kd=T_(f"rkd{i}", [64, 8, 128], BF16),
                                     sm=T_(f"rsm{i}", [64, 24]), gl=T_(f"rgl{i}", [128, 8])))
                tmps = []
                for i in range(NPAR):
                    ga_ = T_(f"Gm{i}", [64, 8, 64])
                    tmps.append(dict(Gm=ga_, arg=ga_, Dm=T_(f"Dm{i}", [64, 8, 64]),
                                     egr=T_(f"egr{i}", [128, 8, 64]), t1=T_(f"t1{i}", [64, 8, 64]),
                                     NTb=T_(f"NTb{i}", [64, 8, 64], BF16), Nb=T_(f"Nb{i}", [64, 8, 64], BF16),
                                     Y=[T_(f"Y{i}_{j}", [64, 8, 64], BF16) for j in range(2)],
                                     P=[T_(f"P{i}_{j}", [64, 8, 64], BF16) for j in range(2)],
                                     Pt=[T_(f"Pt{i}_{j}", [64, 8, 64], BF16) for j in range(2)]))
                S, b_S = T_("S", [128, 8, 128]); Sb, b_Sb = T_("Sb", [128, 8, 128], BF16)
                tk, b_tk = T_("tk", [64, 8, 128])
                rr, b_rr = T_("rr", [64, 8, 128], BF16)
                dl, b_dl = T_("dl", [64, 8, 128], BF16)
                ost = [T_(f"ost{i}", [64, D]) for i in range(1)]
                ot, b_ot = T_("ot", [64, 8, 128])
                sqo, b_sqo = tk, b_tk
                ss, b_ss = T_("ss", [64, 8])
                on, b_on = T_("on", [64, 8, 128], BF16)
                ogT, b_ogT = T_("ogT", [128, 8, 512], BF16)
                xp = [T_(f"gxp{i}", [128, 512]) for i in range(2)]
                xpi = [0]
                ID64 = cst[0:64, C_ID, 0:64]
                IDB64 = cstb[0:64, C_ID, 0:64]
                def bc_h(ap2d):
                    return ap2d.unsqueeze(1).to_broadcast([64, 8, 64])

                def v3(ap, h=8):
                    return ap.rearrange("p (h i) -> p h i", h=h)

                def load_group(d, g):
                    G = grp[g % 2]
                    t0 = g * GW
                    tl = t0 // 512
                    dma("sp", G["q"][0][:], qT_s[:, :, t0:t0 + GW].rearrange("c p t -> p c t"), reads=[b_qs[tl]], writes=[G["q"][1]])
                    dma("sp", G["k"][0][:], kT_s[:, :, t0:t0 + GW].rearrange("c p t -> p c t"), reads=[b_ks[tl]], writes=[G["k"][1]])
                    dma("sp", G["ktm"][0][:], ktm_s[t0:t0 + GW, :].rearrange("(c p) x -> p c x", p=64), reads=[b_ktm[tl]], writes=[G["ktm"][1]])
                    dma("sp", G["vtm"][0][:], vtm_s[t0:t0 + GW, :].rearrange("(c p) x -> p c x", p=64), reads=[b_vtm[tl]], writes=[G["vtm"][1]])
                    dma("sp", G["gb"][0][:], gb_s[t0:t0 + GW, :].rearrange("(c p) x -> p c x", p=64), reads=[b_gbs[tl]], writes=[G["gb"][1]])
                    if d == 1:
                        dma("sp", G["z"][0][:], zT_s[:, :, t0:t0 + GW].rearrange("c p t -> p c t"), reads=[b_zs[tl]], writes=[G["z"][1]])

                def c1_gen(d, c, chain):
                    MC, NM, STR, REV = ((C_MCF, C_NMF, C_M2B, C_M2F) if d == 0 else (C_MCB, C_NMB, C_M2F, C_M2B))
                    mc = cst[0:64, MC, 0:64]; nm = cst[0:64, NM, 0:64]; st_ = cst[0:64, STR, 0:64]; rv = cst[0:64, REV, 0:64]
                    g = c // GC; ci = c % GC
                    G = grp[g % 2]
                    qTg, b_qTg = G["q"]; kTg, b_kTg = G["k"]; ktm, b_ktmg = G["ktm"]; gbg, b_gbg = G["gb"]
                    cs_ = slice(ci * 64, ci * 64 + 64)
                    Tm = tmps[chain]; Rg = ring[c % R]
                    pb = pbank[chain]; bpb = b_pb[chain]
                    Gm, b_Gm = Tm["Gm"]; arg, b_arg = Tm["arg"]; Dm, b_Dm = Tm["Dm"]; egr, b_egr = Tm["egr"]; t1, b_t1 = Tm["t1"]
                    NTb, b_NTb = Tm["NTb"]; Nb, b_Nb = Tm["Nb"]
                    sm, b_sm = Rg["sm"]; gl, b_gl = Rg["gl"]; qg, b_qg = Rg["qg"]; kd, b_kd = Rg["kd"]; QKd, b_QKd = Rg["QKd"]
                    g_c = gbg[:, ci, d * 8:(d + 1) * 8]
                    be_c = gbg[:, ci, 16 + d * 8:16 + (d + 1) * 8]
                    op("dve", lambda e: e.tensor_tensor(out=Gm[:], in0=bc_h(mc), in1=g_c.unsqueeze(2).to_broadcast([64, 8, 64]), op=ALU.mult),
                       reads=[b_gbg, b_cst], writes=[b_Gm])
                    yield
                    op("pe", lambda e: e.matmul(pb[0:64, 0:8], lhsT=mc, rhs=g_c, start=True, stop=True), reads=[b_gbg, b_cst], writes=[bpb])
                    op("pe", lambda e: e.matmul(pb[0:64, 8:16], lhsT=rv, rhs=g_c, start=True, stop=True), reads=[b_gbg, b_cst], writes=[bpb])
                    op("pe", lambda e: e.matmul(pb[:, 16:24], lhsT=cst[0:64, C_ONES, :], rhs=g_c, start=True, stop=True), reads=[b_gbg, b_cst], writes=[bpb])
                    yield
                    op("act", lambda e: e.activation(out=sm[:, 0:16], in_=pb[0:64, 0:16], func=AF.Exp), reads=[bpb], writes=[b_sm])
                    op("act", lambda e: e.activation(out=gl[:], in_=pb[:, 16:24], func=AF.Exp), reads=[bpb], writes=[b_gl])
                    op("act", lambda e: e.copy(out=sm[:, 16:24], in_=pb[0:64, 0:8]), reads=[bpb], writes=[b_sm])
                    yield
                    op("pool", lambda e: e.tensor_tensor(out=kd[:], in0=ktm[:, ci, :].rearrange("p (h v) -> p h v", h=8),
                                                         in1=sm[:, 8:16].unsqueeze(2).to_broadcast([64, 8, 128]), op=ALU.mult),
                       reads=[b_ktmg, b_sm], writes=[b_kd])
                    op("pe", lambda e: e.matmul(pb[:, :], lhsT=cst[0:64, C_ONES, :], rhs=Gm[:].rearrange("p h i -> p (h i)"), start=True, stop=True),
                       reads=[b_Gm, b_cst], writes=[bpb])
                    yield
                    op("dve", lambda e: e.tensor_tensor(out=arg[:], in0=v3(pb[0:64, :]), in1=bc_h(nm), op=ALU.add), reads=[bpb, b_cst], writes=[b_arg])
                    op("act", lambda e: e.activation(out=egr[:], in_=v3(pb[:, :]), func=AF.Exp), reads=[bpb], writes=[b_egr])
                    yield
                    op("pool", lambda e: e.tensor_tensor(out=arg[:], in0=arg[:], in1=sm[:, 16:24].unsqueeze(2).to_broadcast([64, 8, 64]), op=ALU.subtract),
                       reads=[b_sm], writes=[b_arg])
                    for h in range(8):
                        op("pe", lambda e, h=h: e.matmul(pb[0:64, h * 64:(h + 1) * 64], lhsT=kTg[:, h, cs_], rhs=kTg[:, h, cs_], start=True, stop=True),
                           reads=[b_kTg], writes=[bpb])
                    yield
                    op("act", lambda e: e.activation(out=Dm[:], in_=arg[:], func=AF.Exp), reads=[b_arg], writes=[b_Dm])
                    op("dve", lambda e: e.tensor_tensor(out=qg[:], in0=qTg[:, :, cs_], in1=egr[:], op=ALU.mult), reads=[b_qTg, b_egr], writes=[b_qg])
                    yield
                    op("dve", lambda e: e.tensor_tensor(out=t1[:], in0=v3(pb[0:64, :]), in1=Dm[:], op=ALU.mult), reads=[bpb, b_Dm], writes=[b_t1])
                    yield
                    for h in range(8):
                        op("pe", lambda e, h=h: e.matmul(pb[0:64, h * 64:(h + 1) * 64], lhsT=kTg[:, h, cs_], rhs=qTg[:, h, cs_], start=True, stop=True),
                           reads=[b_kTg, b_qTg], writes=[bpb])
                    op("pool", lambda e: e.tensor_tensor(out=t1[:], in0=t1[:], in1=bc_h(st_), op=ALU.mult), reads=[b_cst], writes=[b_t1])
                    yield
                    op("dve", lambda e: e.tensor_tensor(out=QKd[:], in0=v3(pb[0:64, :]), in1=Dm[:], op=ALU.mult), reads=[bpb, b_Dm], writes=[b_QKd])
                    op("dve", lambda e: e.tensor_tensor(out=t1[:], in0=t1[:], in1=be_c.unsqueeze(2).to_broadcast([64, 8, 64]), op=ALU.mult),
                       reads=[b_gbg], writes=[b_t1])
                    yield
                    op("act", lambda e: e.copy(out=NTb[:], in_=t1[:]), reads=[b_t1], writes=[b_NTb])
                    Y0, bY0 = Tm["Y"][0]
                    op("pool", lambda e: e.tensor_tensor(out=Y0[:], in0=bc_h(ID64), in1=t1[:], op=ALU.subtract), reads=[b_t1, b_cst], writes=[bY0])
                    yield
                    tp = pb[:].bitcast(BF16)
                    for h in range(8):
                        op("pe", lambda e, h=h: e.transpose(out=tp[0:64, h * 64:(h + 1) * 64], in_=NTb[:, h, :], identity=IDB64),
                           reads=[b_NTb, b_cstb], writes=[bpb])
                    yield
                    op("act", lambda e: e.copy(out=Nb[:], in_=v3(tp[0:64, 0:512])), reads=[bpb], writes=[b_Nb])
                    yield
                    Pc, bPc = Nb, b_Nb
                    Ptc, bPtc = NTb, b_NTb
                    Yc, bYc = Y0, bY0
                    for lev in range(5):
                        Pn, bPn = Tm["P"][lev % 2]; Ptn, bPtn = Tm["Pt"][lev % 2]
                        Yn, bYn = (Tm["Y"][(lev + 1) % 2] if lev < 4 else Rg["Y"])
                        for h in range(8):
                            op("pe", lambda e, h=h, Pc=Pc, Ptc=Ptc: e.matmul(pb[0:64, h * 64:(h + 1) * 64], lhsT=Ptc[:, h, :], rhs=Pc[:, h, :], start=True, stop=True),
                               reads=[bPc, bPtc], writes=[bpb])
                        yield
                        op("act", lambda e, Pn=Pn: e.copy(out=Pn[:], in_=v3(pb[0:64, :])), reads=[bpb], writes=[bPn])
                        yield
                        if lev < 4:
                            for h in range(8):
                                op("pe", lambda e, h=h, Pc=Pc, Ptc=Ptc: e.matmul(pb[0:64, h * 64:(h + 1) * 64], lhsT=Pc[:, h, :], rhs=Ptc[:, h, :], start=True, stop=True),
                                   reads=[bPc, bPtc], writes=[bpb])
                            yield
                            op("act", lambda e, Ptn=Ptn: e.copy(out=Ptn[:], in_=v3(pb[0:64, :])), reads=[bpb], writes=[bPtn])
                            yield
                        for h in range(8):
                            op("pe", lambda e, h=h, Pn=Pn, Yc=Yc: e.matmul(pb[0:64, h * 64:(h + 1) * 64], lhsT=Pn[:, h, :], rhs=Yc[:, h, :], start=True, stop=True),
                               reads=[bPn, bYc], writes=[bpb])
                        yield
                        op("dve", lambda e, Yn=Yn, Yc=Yc: e.tensor_tensor(out=Yn[:], in0=v3(pb[0:64, :]), in1=Yc[:], op=ALU.add), reads=[bpb, bYc], writes=[bYn])
                        yield
                        Pc, bPc, Ptc, bPtc, Yc, bYc = Pn, bPn, Ptn, bPtn, Yn, bYn

                def scan_gen(d, c):
                    g = c // GC; ci = c % GC
                    G = grp[g % 2]
                    kTg, b_kTg = G["k"]; vtm, b_vtmg = G["vtm"]; gbg, b_gbg = G["gb"]; zTg, b_zTg = G["z"]
                    cs_ = slice(ci * 64, ci * 64 + 64)
                    Rg = ring[c % R]
                    Yc, bYc = Rg["Y"]; QKd, b_QKd = Rg["QKd"]; qg, b_qg = Rg["qg"]; kd, b_kd = Rg["kd"]; sm, b_sm = Rg["sm"]; gl, b_gl = Rg["gl"]
                    be_c = gbg[:, ci, 16 + d * 8:16 + (d + 1) * 8]
                    tok0 = c * 64
                    if (d == 0 and tok0 == SL) or (d == 1 and tok0 == SL - 64):
                        op("pool", lambda e: e.tensor_scalar(out=S[:], in0=S[:], scalar1=flg[:, 0:1], scalar2=None, op0=ALU.mult),
                           reads=[b_flg], writes=[b_S])
                        op("act", lambda e: e.copy(out=Sb[:], in_=S[:]), reads=[b_S], writes=[b_Sb])
                    ost_, b_ost = ost[0]
                    if d == 1:
                        dma("act", ost_[:], o_s[tok0:tok0 + 64, :], reads=[b_os[tok0 // 512]], writes=[b_ost])
                    for h in range(8):
                        op("pe", lambda e, h=h: e.matmul(pbank[4 + h // 4][0:64, (h % 4) * 128:(h % 4 + 1) * 128], lhsT=kTg[:, h, cs_], rhs=Sb[:, h, :],
                                                         start=True, stop=True), reads=[b_kTg, b_Sb], writes=[b_pb[4 + h // 4]])
                    yield
                    for hh in range(2):
                        op("dve", lambda e, hh=hh: e.tensor_tensor(out=tk[:, hh * 4:hh * 4 + 4, :], in0=v3(pbank[4 + hh][0:64, :], 4),
                                                                   in1=sm[:, hh * 4:hh * 4 + 4].unsqueeze(2).to_broadcast([64, 4, 128]), op=ALU.mult),
                           reads=[b_pb[4 + hh], b_sm], writes=[b_tk])
                    yield
                    op("pool", lambda e: e.tensor_tensor(out=rr[:], in0=vtm[:, ci, :].rearrange("p (h v) -> p h v", h=8), in1=tk[:], op=ALU.subtract),
                       reads=[b_vtmg, b_tk], writes=[b_rr])
                    yield
                    for h in range(8):
                        op("pe", lambda e, h=h: e.matmul(pbank[4 + h // 4][0:64, (h % 4) * 128:(h % 4 + 1) * 128], lhsT=Yc[:, h, :], rhs=rr[:, h, :],
                                                         start=True, stop=True), reads=[bYc, b_rr], writes=[b_pb[4 + h // 4]])
                    yield
                    op("dve", lambda e: e.tensor_tensor(out=dl[:, 0:4, :], in0=v3(pbank[4][0:64, :], 4),
                                                        in1=be_c[:, 0:4].unsqueeze(2).to_broadcast([64, 4, 128]), op=ALU.mult),
                       reads=[b_pb[4], b_gbg], writes=[b_dl])
                    op("pool", lambda e: e.tensor_scalar(out=S[:], in0=S[:], scalar1=1.0, scalar2=None, op0=ALU.mult) if False else
                       e.tensor_tensor(out=S[:], in0=S[:], in1=gl[:].unsqueeze(2).to_broadcast([128, 8, 128]), op=ALU.mult),
                       reads=[b_gl], writes=[b_S])
                    op("dve", lambda e: e.tensor_tensor(out=dl[:, 4:8, :], in0=v3(pbank[5][0:64, :], 4),
                                                        in1=be_c[:, 4:8].unsqueeze(2).to_broadcast([64, 4, 128]), op=ALU.mult),
                       reads=[b_pb[5], b_gbg], writes=[b_dl])
                    yield
                    for h in range(8):
                        op("pe", lambda e, h=h: e.matmul(pbank[4 + h // 4][:, (h % 4) * 128:(h % 4 + 1) * 128], lhsT=kd[:, h, :], rhs=dl[:, h, :],
                                                         start=True, stop=True), reads=[b_kd, b_dl], writes=[b_pb[4 + h // 4]])
                    for h in range(8):
                        pbi = 6 + h // 4
                        osl = pbank[pbi][0:64, (h % 4) * 128:(h % 4 + 1) * 128]
                        op("pe", lambda e, h=h, osl=osl: e.matmul(osl, lhsT=qg[:, h, :], rhs=Sb[:, h, :], start=True, stop=False),
                           reads=[b_qg, b_Sb], writes=[b_pb[pbi]])
                        op("pe", lambda e, h=h, osl=osl: e.matmul(osl, lhsT=QKd[:, h, :], rhs=dl[:, h, :], start=False, stop=True),
                           reads=[b_QKd, b_dl], writes=[b_pb[pbi]])
                    yield
                    for hh in range(2):
                        op("dve", lambda e, hh=hh: e.tensor_tensor(out=S[:, hh * 4:hh * 4 + 4, :], in0=S[:, hh * 4:hh * 4 + 4, :],
                                                                   in1=v3(pbank[4 + hh][:, :], 4), op=ALU.add),
                           reads=[b_pb[4 + hh]], writes=[b_S])
                    yield
                    op("act", lambda e: e.copy(out=Sb[:], in_=S[:]), reads=[b_S], writes=[b_Sb])
                    if d == 0:
                        for hh in range(2):
                            op("act", lambda e, hh=hh: e.copy(out=ost_[:, hh * 512:(hh + 1) * 512], in_=pbank[6 + hh][0:64, :]),
                               reads=[b_pb[6 + hh]], writes=[b_ost])
                        dma("act", o_s[tok0:tok0 + 64, :], ost_[:], reads=[b_ost], writes=[b_os[tok0 // 512]], par=True)
                        yield
                    else:
                        for hh in range(2):
                            op("dve", lambda e, hh=hh: e.tensor_tensor(out=ot[:, hh * 4:hh * 4 + 4, :], in0=v3(pbank[6 + hh][0:64, :], 4),
                                                                       in1=v3(ost_[:, hh * 512:(hh + 1) * 512], 4), op=ALU.add),
                               reads=[b_pb[6 + hh], b_ost], writes=[b_ot])
                        yield
                        op("pool", lambda e: e.tensor_tensor(out=sqo[:], in0=ot[:], in1=ot[:], op=ALU.mult), reads=[b_ot], writes=[b_sqo])
                        yield
                        op("dve", lambda e: e.tensor_reduce(out=ss[:], in_=sqo[:], axis=AX.X, op=ALU.add), reads=[b_sqo], writes=[b_ss])
                        yield
                        op("act", lambda e: e.activation(out=ss[:], in_=ss[:], func=AF.Ln, bias=epsc[0:64, 0:1], scale=1.0 / 128.0),
                           reads=[b_eps], writes=[b_ss])
                        op("act", lambda e: e.activation(out=ss[:], in_=ss[:], func=AF.Exp, scale=-0.5), writes=[b_ss])
                        yield
                        op("dve", lambda e: e.tensor_tensor(out=on[:], in0=ot[:], in1=ss[:].unsqueeze(2).to_broadcast([64, 8, 128]), op=ALU.mult),
                           reads=[b_ot, b_ss], writes=[b_on])
                        yield
                        tpo = pbank[6][:].bitcast(BF16)
                        for h in range(8):
                            op("pe", lambda e, h=h: e.transpose(out=tpo[:, h * 64:(h + 1) * 64], in_=on[:, h, :], identity=IDB64),
                               reads=[b_on, b_cstb], writes=[b_pb[6]])
                        yield
                        c8 = (tok0 % 512) // 64
                        op("dve", lambda e: e.scalar_tensor_tensor(out=ogT[:, :, c8 * 64:(c8 + 1) * 64], in0=v3(tpo[:, 0:512]),
                                                                   scalar=gng[:, 0:1], in1=zTg[:, :, cs_], op0=ALU.mult, op1=ALU.mult),
                           reads=[b_pb[6], b_gng, b_zTg], writes=[b_ogT])
                        yield
                        if tok0 % 512 == 0:
                            t0 = tok0
                            tl = t0 // 512
                            slot = t0 // SL
                            for dc in range(NCH):
                                xp_, bxp = xp[xpi[0] % 2]; xpi[0] += 1
                                dma("sp", xp_[:], xT[dc, :, t0:t0 + 512], reads=[b_xTt[tl][dc]], writes=[bxp])
                                for h in range(8):
                                    op("pe", lambda e, h=h, dc=dc: e.matmul(pbank[7][:, :], lhsT=wout[:, h, dc * 128:(dc + 1) * 128], rhs=ogT[:, h, :],
                                                                           start=(h == 0), stop=(h == 7)), reads=[b_wout, b_ogT], writes=[b_pb[7]])
                                yield
                                op("dve", lambda e, dc=dc, xp_=xp_, slot=slot: e.scalar_tensor_tensor(
                                    out=xp_[:], in0=pbank[7][:, :], scalar=mods[:, l, g1idx + dc, slot:slot + 1], in1=xp_[:], op0=ALU.mult, op1=ALU.add),
                                   reads=[b_pb[7], b_mods], writes=[bxp])
                                dma("act", xT[dc, :, t0:t0 + 512], xp_[:], reads=[bxp], writes=[b_xTt[tl][dc]])
                                yield

                for d in range(int(os.environ.get("KDBG_ND", "2"))):
                    op("pool", lambda e: e.memset(S[:], 0.0), writes=[b_S])
                    op("pool", lambda e: e.memset(Sb[:], 0.0), writes=[b_Sb])
                    order = list(range(NCHK)) if d == 0 else list(range(NCHK - 1, -1, -1))
                    nxt_c1 = 0
                    c1_done = 0
                    scan_done = 0
                    active = {}
                    finished = set()
                    scan_g = None
                    loaded = set()
                    ycount = {}
                    while scan_done < NCHK:
                        for chain in range(NPAR):
                            if chain not in active and nxt_c1 < NCHK and nxt_c1 < scan_done + R:
                                c = order[nxt_c1]
                                g = c // GC
                                if g not in loaded:
                                    prev_needed = nxt_c1 - GC
                                    if scan_done < max(0, nxt_c1 - GC):
                                        continue
                                    load_group(d, g); loaded.add(g)
                                active[chain] = (nxt_c1, c1_gen(d, c, chain))
                                nxt_c1 += 1
                        for chain in list(active.keys()):
                            pos, gen_ = active[chain]
                            try:
                                next(gen_)
                                ycount[pos] = ycount.get(pos, 0) + 1
                                if ycount[pos] >= int(os.environ.get("KDBG_C1STOP", "1000")):
                                    raise StopIteration
                            except StopIteration:
                                finished.add(pos)
                                del active[chain]
                        while c1_done in finished:
                            c1_done += 1
                        if scan_g is None and scan_done < c1_done:
                            scan_g = scan_gen(d, order[scan_done]) if not os.environ.get("KDBG_NOSCAN") else iter(())
                        if scan_g is not None:
                            try:
                                next(scan_g)
                            except StopIteration:
                                scan_g = None
                                scan_done += 1
                k.end_phase(pbufs)

        fi_cnt = 0
        gi_cnt = 0
        for l in range(L):
            lt = layer_types[l]
            import os
            if os.environ.get("KDBG_SKIP_MIXER"):
                lt = "skip"
            if lt == "skip":
                pass
            elif lt == "f":
                fi = fi_cnt; fi_cnt += 1
                with contextlib.ExitStack() as ph:
                    pbufs = []
                    def pbuf(n):
                        b = k.buf(n); pbufs.append(b); return b
                    U = sb(ph, "U", [128, NB, D], BF16); V = sb(ph, "V", [128, NB, D], BF16)
                    b_U = [pbuf(f"U{i}") for i in range(NB)]
                    with contextlib.ExitStack() as ph1:
                        p1 = []
                        def pbuf1(n):
                            b = k.buf(n); p1.append(b); return b
                        xt = [sb(ph1, f"fxt{i}", [128, NCH, 512]) for i in range(2)]; bxt = [pbuf1(f"fxt{i}") for i in range(2)]
                        hts = [sb(ph1, f"fht{i}", [128, NCH, 512], BF16) for i in range(2)]; bhts = [pbuf1(f"fht{i}") for i in range(2)]
                        RG = mk_ring(ph1, pbuf1, "f", nsq=8)

                        def f_norm_gen(t):
                            x_ = xt[t % 2]; bx = bxt[t % 2]
                            dma("sp", x_[:], xT[:, :, t * 512:(t + 1) * 512].rearrange("c p t -> p c t"), reads=b_xTt[t], writes=[bx])
                            yield
                            yield from norm_mod_gen(x_, bx, hts[t % 2], bhts[t % 2], RG, l, 48, 0, (t * 512) // SL, 0)
                        cd = sb(ph1, "cd", [128, 256], BF16); bcd = pbuf1("cd")
                        dma("sp", cd[:], cdft, writes=[bcd])
                        for _ in f_norm_gen(0):
                            pass
                        for t in range(NTT):
                            ht = hts[t % 2]; bht = bhts[t % 2]
                            nxt = f_norm_gen(t + 1) if t + 1 < NTT else iter(())
                            for s in range(4):
                                tb = t * 4 + s
                                for cs_ in range(2):
                                    for half in range(2):
                                        next(nxt, None)
                                        next(nxt, None)
                                        pb_i = 1 + (s * 4 + cs_ * 2 + half) % 6
                                        for g4 in range(4):
                                            g = half * 4 + g4
                                            op("pe", lambda e, g=g, g4=g4, s=s, cs_=cs_, pb_i=pb_i: e.matmul(
                                                pbank[pb_i][:, g4 * 128:(g4 + 1) * 128], lhsT=ht[:, g, s * 128:(s + 1) * 128],
                                                rhs=cd[:, cs_ * 128:(cs_ + 1) * 128], start=True, stop=True),
                                               reads=[bht, bcd], writes=[b_pb[pb_i]])
                                        dst = (U if cs_ == 0 else V)
                                        if (cs_ + half) % 2 == 0:
                                            op("act", lambda e, dst=dst, tb=tb, half=half, pb_i=pb_i: e.copy(
                                                out=dst[:, tb, half * 512:(half + 1) * 512], in_=pbank[pb_i][:, :]),
                                               reads=[b_pb[pb_i]], writes=[b_U[tb]])
                                        else:
                                            op("dve", lambda e, dst=dst, tb=tb, half=half, pb_i=pb_i: e.tensor_copy(
                                                out=dst[:, tb, half * 512:(half + 1) * 512], in_=pbank[pb_i][:, :]),
                                               reads=[b_pb[pb_i]], writes=[b_U[tb]])
                            for _ in nxt:
                                pass
                        k.end_phase(p1)
                    fw = sb(ph, "fw", [128, NCH, D], BF16); bfw = pbuf("fw")
                    for kc in range(NCH):
                        dma("pool", fw[:, kc, :], fnet_w[fi, kc * 128:(kc + 1) * 128, :], writes=[bfw], par=(kc > 0))
                    NR = 3
                    mring = [sb(ph, f"mr{i}", [128, NB, 128], BF16) for i in range(NR)]
                    b_mr = [pbuf(f"mr{i}") for i in range(NR)]
                    mtm = [sb(ph, f"mtm{i}", [128, D], BF16) for i in range(2)]; b_mtm = [pbuf(f"mtm{i}") for i in range(2)]
                    mT = [sb(ph, f"mT{i}", [128, NCH, 512], BF16) for i in range(1)]; b_mT = [pbuf(f"mT{i}") for i in range(1)]
                    xp = [sb(ph, f"xp{i}", [128, 512]) for i in range(3)]; b_xp = [pbuf(f"xp{i}") for i in range(3)]
                    yt = [sb(ph, f"yt{i}", [128, 512]) for i in range(2)]; b_yt = [pbuf(f"yt{i}") for i in range(2)]
                    xpi = 0
                    xpi_l = [0]

                    def post(m):
                        mt = mtm[m % 2]; bmt = b_mtm[m % 2]
                        tpi = 5 + (m % 2)
                        tp = pbank[tpi][:].bitcast(BF16)
                        for c in range(NCH):
                            op("pe", lambda e, mt=mt, c=c, tp=tp: e.transpose(out=tp[:, c * 128:(c + 1) * 128],
                                                                              in_=mt[:, c * 128:(c + 1) * 128], identity=ident_b),
                               reads=[bmt, b_cstb], writes=[b_pb[tpi]])
                        g4 = m // 4
                        mT_ = mT[0]; bmT = b_mT[0]
                        op("act" if m % 2 == 0 else "dve",
                           (lambda e, mT_=mT_, tp=tp, m=m: e.copy(out=mT_[:, :, (m % 4) * 128:(m % 4 + 1) * 128],
                                                                 in_=tp.rearrange("p (c t) -> p c t", c=NCH))) if m % 2 == 0 else
                           (lambda e, mT_=mT_, tp=tp, m=m: e.tensor_copy(out=mT_[:, :, (m % 4) * 128:(m % 4 + 1) * 128],
                                                                        in_=tp.rearrange("p (c t) -> p c t", c=NCH))),
                           reads=[b_pb[tpi]], writes=[bmT])
                        if m % 4 == 3:
                            t0 = (m // 4) * 512
                            slot = t0 // SL
                            for dc in range(NCH):
                                xp_ = xp[xpi_l[0] % 3]; bxp = b_xp[xpi_l[0] % 3]; xpi_l[0] += 1
                                dma("pool", xp_[:], xT[dc, :, t0:t0 + 512], reads=[b_xTt[m // 4][dc]], writes=[bxp])
                                ypi = 7 if dc % 2 == 0 else 0
                                for c in range(NCH):
                                    op("pe", lambda e, c=c, dc=dc, mT_=mT_, ypi=ypi: e.matmul(
                                        pbank[ypi][:, :], lhsT=fw[:, c, dc * 128:(dc + 1) * 128], rhs=mT_[:, c, :],
                                        start=(c == 0), stop=(c == NCH - 1)), reads=[bfw, bmT], writes=[b_pb[ypi]])
                                y_ = yt[dc % 2]; by = b_yt[dc % 2]
                                op("act", lambda e, y_=y_, ypi=ypi, dc=dc, slot=slot: e.activation(
                                    out=y_[:], in_=pbank[ypi][:, :], func=AF.Identity,
                                    scale=mods[:, l, 16 + dc, slot:slot + 1], bias=mods[:, l, 64 + dc, slot:slot + 1]),
                                   reads=[b_pb[ypi], b_mods], writes=[by])
                                op("dve", lambda e, y_=y_, xp_=xp_: e.tensor_tensor(out=xp_[:], in0=xp_[:], in1=y_[:], op=ALU.add),
                                   reads=[by], writes=[bxp])
                                dma("pool", xT[dc, :, t0:t0 + 512], xp_[:], reads=[bxp], writes=[b_xTt[m // 4][dc]])

                    def dft_load(idx):
                        m_, which = idx // 2, idx % 2
                        if m_ < NB:
                            dma("sp", mring[idx % NR][:], (dftc if which == 0 else dfts)[m_], writes=[b_mr[idx % NR]])
                    for idx in range(NR):
                        dft_load(idx)
                    for m in range(NB):
                        pa, pb_ = 1 + (m % 2) * 2, 2 + (m % 2) * 2
                        for which, (src, first, last) in enumerate(((U, True, False), (V, False, True))):
                            idx = 2 * m + which
                            mat = mring[idx % NR]; bmat = b_mr[idx % NR]
                            for kk in range(NB):
                                for half in range(2):
                                    pbi = pa if half == 0 else pb_
                                    op("pe", lambda e, mat=mat, kk=kk, half=half, pbi=pbi, src=src, first=first, last=last: e.matmul(
                                        pbank[pbi][:, :], lhsT=mat[:, kk, :], rhs=src[:, kk, half * 512:(half + 1) * 512],
                                        start=(first and kk == 0), stop=(last and kk == NB - 1)),
                                       reads=[bmat, b_U[kk]], writes=[b_pb[pbi]])
                            dft_load(idx + NR)
                            if which == 0 and m >= 1:
                                post(m - 1)
                        mt = mtm[m % 2]; bmt = b_mtm[m % 2]
                        op("act", lambda e, mt=mt, pa=pa: e.copy(out=mt[:, 0:512], in_=pbank[pa][:, :]), reads=[b_pb[pa]], writes=[bmt])
                        op("dve", lambda e, mt=mt, pb_=pb_: e.tensor_copy(out=mt[:, 512:1024], in_=pbank[pb_][:, :]), reads=[b_pb[pb_]], writes=[bmt])
                    post(NB - 1)
                    k.end_phase(pbufs)
            else:
                gi = gi_cnt; gi_cnt += 1
                gdn_layer(l, gi)
            if os.environ.get("KDBG_SKIP_FFN"):
                continue
            with contextlib.ExitStack() as ph:
                pbufs = []
                def pbuf(n):
                    b = k.buf(n); pbufs.append(b); return b
                wgu = sb(ph, "wgu", [128, NCH, 2 * FF], BF16); b_wgu = [pbuf(f"wgu{i}") for i in range(11)]
                wd = sb(ph, "wd", [128, NFC, D], BF16); b_wd = [pbuf(f"wd{i}") for i in range(NFC)]
                order = [0, 5, 6, 1, 7, 2, 8, 3, 9, 4, 10]
                for blk in order:
                    for kc in range(NCH):
                        dma("pool", wgu[:, kc, blk * 512:(blk + 1) * 512],
                            ffn_w_gu[l, kc * 128:(kc + 1) * 128, blk * 512:(blk + 1) * 512], writes=[b_wgu[blk]], par=(kc > 0))
                for fc in range(NFC):
                    dma("pool", wd[:, fc, :], ffn_w_down[l, fc * 128:(fc + 1) * 128, :], writes=[b_wd[fc]])
                xt = sb(ph, "xt", [128, NCH, 512]); bx = pbuf("xt")
                xp = [sb(ph, f"xp{i}", [128, 512]) for i in range(2)]; b_xp = [pbuf(f"xp{i}") for i in range(2)]
                hts = [sb(ph, f"ht{i}", [128, NCH, 512], BF16) for i in range(2)]; bhts = [pbuf(f"ht{i}") for i in range(2)]
                RG = mk_ring(ph, pbuf, "n", with_tmp=False)

                def ffn_norm_gen(t):
                    dma("sp", xt[:], xT[:, :, t * 512:(t + 1) * 512].rearrange("c p t -> p c t"), reads=b_xTt[t], writes=[bx])
                    yield
                    yield from norm_mod_gen(xt, bx, hts[t % 2], bhts[t % 2], RG, l, 56, 24, (t * 512) // SL, 0, inplace=True)
                act = sb(ph, "act", [128, NFC, 512], BF16); b_act = [pbuf(f"act{i}") for i in range(NFC)]
                sg = [sb(ph, f"sg{i}", [128, 512]) for i in range(2)]; b_sg = [pbuf(f"sg{i}") for i in range(2)]
                xpi = 0
                for _ in ffn_norm_gen(0):
                    pass
                for t in range(NTT):
                    slot = (t * 512) // SL
                    ht = hts[t % 2]; bht = bhts[t % 2]
                    nxt = ffn_norm_gen(t + 1) if t + 1 < NTT else iter(())
                    for j in range(NFC):
                        if j >= 2:
                            next(nxt, None)
                        pg, pu = 1 + (j % 2) * 2, 2 + (j % 2) * 2
                        for (pbi, col0) in ((pg, j * 128), (pu, FF + j * 128)):
                            blk = col0 // 512
                            for kc in range(NCH):
                                op("pe", lambda e, pbi=pbi, col0=col0, kc=kc: e.matmul(
                                    pbank[pbi][:, :], lhsT=wgu[:, kc, col0:col0 + 128], rhs=ht[:, kc, :],
                                    start=(kc == 0), stop=(kc == NCH - 1)), reads=[b_wgu[blk], bht], writes=[b_pb[pbi]])
                        s_ = sg[j % 2]; bs = b_sg[j % 2]
                        op("act", lambda e, s_=s_, pg=pg: e.activation(out=s_[:], in_=pbank[pg][:, :], func=AF.Silu),
                           reads=[b_pb[pg]], writes=[bs])
                        op("dve", lambda e, s_=s_, pu=pu, j=j: e.tensor_tensor(out=act[:, j, :], in0=pbank[pu][:, :], in1=s_[:], op=ALU.mult),
                           reads=[b_pb[pu], bs], writes=[b_act[j]])
                    for dc in range(NCH):
                        ypi = 5 + dc % 2
                        xp_ = xp[xpi % 2]; bxp = b_xp[xpi % 2]; xpi += 1
                        dma("pool", xp_[:], xT[dc, :, t * 512:(t + 1) * 512], reads=[b_xTt[t][dc]], writes=[bxp])
                        if dc == 0:
                            for _ in nxt:
                                pass
                        for j in range(NFC):
                            op("pe", lambda e, ypi=ypi, j=j, dc=dc: e.matmul(
                                pbank[ypi][:, :], lhsT=wd[:, j, dc * 128:(dc + 1) * 128], rhs=act[:, j, :],
                                start=(j == 0), stop=(j == NFC - 1)), reads=[b_wd[j], b_act[j]], writes=[b_pb[ypi]])
                        op("dve", lambda e, ypi=ypi, dc=dc, xp_=xp_, slot=slot: e.scalar_tensor_tensor(
                            out=xp_[:], in0=pbank[ypi][:, :], scalar=mods[:, l, 40 + dc, slot:slot + 1], in1=xp_[:],
                            op0=ALU.mult, op1=ALU.add), reads=[b_pb[ypi], b_mods], writes=[bxp])
                        dma("pool", xT[dc, :, t * 512:(t + 1) * 512], xp_[:], reads=[bxp], writes=[b_xTt[t][dc]])
                k.end_phase(pbufs)

        with contextlib.ExitStack() as ph:
            pbufs = []
            def pbuf(n):
                b = k.buf(n); pbufs.append(b); return b
            xt = [sb(ph, f"xt{i}", [128, NCH, 512]) for i in range(2)]; bxt = [pbuf(f"xt{i}") for i in range(2)]
            hfs = [sb(ph, f"hf{i}", [128, NCH, 512]) for i in range(2)]; bhfs = [pbuf(f"hf{i}") for i in range(2)]
            RG = mk_ring(ph, pbuf, "z", nsq=8)
            yo = [sb(ph, f"yo{i}", [128, D]) for i in range(2)]; byo = [pbuf(f"yo{i}") for i in range(2)]

            def fin_norm_gen(t):
                x_ = xt[t % 2]; bx = bxt[t % 2]
                dma("sp", x_[:], xT[:, :, t * 512:(t + 1) * 512].rearrange("c p t -> p c t"), reads=b_xTt[t], writes=[bx])
                yield
                yield from norm_mod_gen(x_, bx, hfs[t % 2], bhfs[t % 2], RG, L, 48, 0, (t * 512) // SL, 0)

            for _ in fin_norm_gen(0):
                pass
            for t in range(NTT):
                hf = hfs[t % 2]; bhf = bhfs[t % 2]
                nxt = fin_norm_gen(t + 1) if t + 1 < NTT else iter(())
                for s in range(4):
                    tb = t * 4 + s
                    y_ = yo[tb % 2]; by = byo[tb % 2]
                    for half in range(2):
                        next(nxt, None); next(nxt, None); next(nxt, None)
                        pbi = 1 + (tb * 2 + half) % 6
                        for c4 in range(4):
                            c = half * 4 + c4
                            op("pe", lambda e, c=c, c4=c4, s=s, pbi=pbi: e.transpose(
                                out=pbank[pbi][:, c4 * 128:(c4 + 1) * 128], in_=hf[:, c, s * 128:(s + 1) * 128], identity=ident_f),
                               reads=[bhf, b_cst], writes=[b_pb[pbi]])
                        if half == 0:
                            op("act", lambda e, y_=y_, pbi=pbi: e.copy(out=y_[:, 0:512], in_=pbank[pbi][:, :]), reads=[b_pb[pbi]], writes=[by])
                        else:
                            op("dve", lambda e, y_=y_, pbi=pbi: e.tensor_copy(out=y_[:, 512:1024], in_=pbank[pbi][:, :]), reads=[b_pb[pbi]], writes=[by])
                    dma("pool", y_out[tb * 128:(tb + 1) * 128, :], y_[:], reads=[by], writes=[])
                for _ in nxt:
                    pass
            k.end_phase(pbufs)
        if os.environ.get("KDBG_MODS"):
            dbg = nc.dram_tensor("dbg", [128, (L + 1) * 144], F32, kind="ExternalOutput").ap()
            dma("sp", dbg, mods[:].rearrange("p l n s -> p (l n s)"), reads=[b_mods])
        k.finish()
    return nc


_CACHE = {}


def _consts():
    c = np.zeros((128, 8, 128), np.float32)
    c[:, 0, :] = np.eye(128)
    c[:, 1, :] = 1.0
    j = np.arange(64)[:, None]; i = np.arange(64)[None, :]
    c[:64, 2, :64] = (j <= i)
    c[:64, 3, :64] = (j >= i)
    c[:64, 4, :64] = (j > i)
    c[:64, 5, :64] = (j < i)
    c[:64, 6, :64] = np.where(j <= i, 0.0, -30000.0)
    c[:64, 7, :64] = np.where(j >= i, 0.0, -30000.0)
    return c


def _dft_mats(NT, nseq):
    S = NT // nseq
    n = np.arange(S)
    ang = 2.0 * np.pi * ((n[:, None] * n[None, :]) % S) / S
    sc = 1.0 / np.sqrt(S)
    Cb = np.cos(ang) * sc
    Sb = -np.sin(ang) * sc
    Mc = np.zeros((NT, NT), np.float32); Ms = np.zeros((NT, NT), np.float32)
    for q in range(nseq):
        Mc[q * S:(q + 1) * S, q * S:(q + 1) * S] = Cb
        Ms[q * S:(q + 1) * S, q * S:(q + 1) * S] = Sb
    NB = NT // 128
    def lay(M):
        return np.ascontiguousarray(M.reshape(NB, 128, NB, 128).transpose(2, 1, 0, 3)).astype(ml_dtypes.bfloat16)
    return lay(Mc), lay(Ms)


def _cdft():
    n = np.arange(128)
    ang = 2.0 * np.pi * ((n[:, None] * n[None, :]) % 128) / 128
    sc = 1.0 / np.sqrt(128.0)
    return np.concatenate([np.cos(ang) * sc, np.sin(ang) * sc], axis=1).astype(ml_dtypes.bfloat16)


def run_cores(core_x, core_c, nseqs, W, NT, layer_types):
    L = len(layer_types)
    key = (NT, tuple(layer_types))
    if key not in _CACHE:
        _CACHE[key] = build(NT, layer_types)
    nc = _CACHE[key]
    f32 = lambda a: np.ascontiguousarray(a, dtype=np.float32)
    colT = lambda v: f32(np.asarray(v).reshape(-1, 128).T)
    shared = {
        "ada_w": f32(W["ada_w"][:L]),
        "ada_bT": f32(np.stack([colT(W["ada_b"][i]) for i in range(L)])),
        "nmgT": f32(np.stack([colT(W["norm_mix_g"][i]) for i in range(L)])),
        "nfgT": f32(np.stack([colT(W["norm_ffn_g"][i]) for i in range(L)])),
        "fada_w": f32(W["final_ada_w"]),
        "fada_bT": colT(W["final_ada_b"]),
        "fngT": colT(W["final_norm_g"]),
        "fnet_w": f32(W["fnet_w"]),
        "fnet_bT": f32(np.stack([colT(W["fnet_b"][i]) for i in range(W["fnet_w"].shape[0])])),
        "gdn_w_in": f32(W["gdn_w_in"]),
        "conv_wT": f32(np.stack([np.asarray(W["gdn_conv_w"][i]).T.reshape(24, 128, 5).transpose(1, 0, 2)
                                 for i in range(W["gdn_w_in"].shape[0])])),
        "alog_row": f32(np.asarray(W["gdn_a_log"]).reshape(-1, 16)),
        "dtb_col": f32(np.concatenate([np.asarray(W["gdn_dt_bias"]).reshape(-1, 16), np.zeros((W["gdn_w_in"].shape[0], 16), np.float32)], axis=1)[:, :, None]),
        "gng_col": f32(np.asarray(W["gdn_norm_g"])[:, :, None]),
        "gdn_w_out": f32(W["gdn_w_out"]),
        "ffn_w_gu": f32(W["ffn_w_gu"][:L]),
        "ffn_w_down": f32(W["ffn_w_down"][:L]),
        "cdft": _cdft(),
        "cf32": _consts(),
    }
    dft = {n: _dft_mats(NT, n) for n in set(nseqs)}
    in_maps = []
    for x, c, n in zip(core_x, core_c, nseqs):
        m = dict(shared)
        m["x_in"] = f32(x)
        m["cT"] = f32(np.asarray(c).reshape(2, NCH, 128).transpose(2, 1, 0))
        m["dftc"], m["dfts"] = dft[n]
        m["flag"] = np.full((128, 1), 1.0 if n == 1 else 0.0, np.float32)
        in_maps.append(m)
    if os.environ.get("KDBG_TRACE"):
        res = run_bass_kernel_spmd(nc, in_maps, core_ids=list(range(len(in_maps))), trace=True)
        print("EXEC_TIME_NS", res.exec_time_ns)
    else:
        res = run_bass_kernel_spmd(nc, in_maps, core_ids=list(range(len(in_maps))))
    if os.environ.get("KDBG_MODS"):
        run_cores.dbg = [r["dbg"] for r in res.results]
    return [r["y_out"] for r in res.results]


def kernel(x_prompt, x_sample, c_prompt, c_sample, ada_w, ada_b, norm_mix_g, norm_ffn_g,
           fnet_w, fnet_b, gdn_w_in, gdn_conv_w, gdn_a_log, gdn_dt_bias, gdn_norm_g, gdn_w_out,
           ffn_w_gu, ffn_w_down, final_ada_w, final_ada_b, final_norm_g):
    W = dict(ada_w=ada_w, ada_b=ada_b, norm_mix_g=norm_mix_g, norm_ffn_g=norm_ffn_g, fnet_w=fnet_w, fnet_b=fnet_b,
             gdn_w_in=gdn_w_in, gdn_conv_w=gdn_conv_w, gdn_a_log=gdn_a_log, gdn_dt_bias=gdn_dt_bias,
             gdn_norm_g=gdn_norm_g, gdn_w_out=gdn_w_out, ffn_w_gu=ffn_w_gu, ffn_w_down=ffn_w_down,
             final_ada_w=final_ada_w, final_ada_b=final_ada_b, final_norm_g=final_norm_g)
    W = {kk: np.asarray(v) for kk, v in W.items()}
    x_prompt = np.asarray(x_prompt); x_sample = np.asarray(x_sample)
    c_prompt = np.asarray(c_prompt); c_sample = np.asarray(c_sample)
    core_x, core_c, nseqs = [], [], []
    for b in range(4):
        core_x.append(x_prompt[b]); core_c.append(np.stack([c_prompt[b], c_prompt[b]])); nseqs.append(1)
    for b in range(4):
        core_x.append(x_sample[2 * b:2 * b + 2].reshape(4096, D)); core_c.append(c_sample[2 * b:2 * b + 2]); nseqs.append(2)
    ys = run_cores(core_x, core_c, nseqs, W, 4096, ["f", "g", "f", "g"])
    y_prompt = np.stack(ys[:4]).astype(np.float32)
    y_sample = np.concatenate([y.reshape(2, 2048, D) for y in ys[4:]], axis=0).astype(np.float32)
    return (y_prompt, y_sample)
```
